# Optimizing a Trainium2 kernel written in Bass

```python
import jax, jax.numpy as jnp
from jax import lax
import numpy as np

D_MODEL = 1024
BATCH = 8
SEQ = 2048
DEPTH = 4

N_HEADS = 8
HEAD_DIM = 64
ATTN_WIDTH = N_HEADS * HEAD_DIM
CONV_CH = 512
CONV_K = 31
D_FF = 2816
D_PLE = 256
Q_BLOCK = 128
EPS = 1e-6
FFN_RES = 0.5

Q0 = 0
K0 = Q0 + ATTN_WIDTH
V0 = K0 + ATTN_WIDTH
F0 = V0 + ATTN_WIDTH
C0 = F0 + N_HEADS
GA0 = C0 + 2 * CONV_CH
GC0 = GA0 + D_MODEL
IN_COLS = GC0 + D_MODEL

kernel_name = "macaron_fox_conformer_hybrid"


def rmsnorm(x, g):
    xf = x.astype(jnp.float32)
    y = xf * lax.rsqrt(jnp.mean(xf * xf, axis=-1, keepdims=True) + EPS)
    return (y * g.astype(jnp.float32)).astype(x.dtype)


def swiglu(x, w_in, w_out):
    a, b = jnp.split(x @ w_in, 2, axis=-1)
    return (jax.nn.silu(a) * b) @ w_out


def forgetting_attention(q, k, v, f_logit):
    b, s, h, dh = q.shape
    scale = 1.0 / float(np.sqrt(dh))
    c = jnp.cumsum(jax.nn.log_sigmoid(f_logit.astype(jnp.float32)), axis=1)
    c = jnp.transpose(c, (0, 2, 1))
    qh = jnp.transpose(q, (0, 2, 1, 3))
    kh = jnp.transpose(k, (0, 2, 1, 3))
    vh = jnp.transpose(v, (0, 2, 1, 3))
    outs = []
    for blk in range(s // Q_BLOCK):
        qs, qe = blk * Q_BLOCK, (blk + 1) * Q_BLOCK
        sc = jnp.einsum('bhqd,bhkd->bhqk', qh[:, :, qs:qe], kh[:, :, :qe]).astype(jnp.float32) * scale
        sc = sc + (c[:, :, qs:qe, None] - c[:, :, None, :qe])
        causal = jnp.arange(qs, qe)[:, None] >= jnp.arange(qe)[None, :]
        sc = jnp.where(causal[None, None], sc, -jnp.inf)
        pr = jax.nn.softmax(sc, axis=-1).astype(vh.dtype)
        outs.append(jnp.einsum('bhqk,bhkd->bhqd', pr, vh[:, :, :qe]))
    o = jnp.concatenate(outs, axis=2)
    return jnp.transpose(o, (0, 2, 1, 3)).reshape(b, s, h * dh)


def conformer_conv(glu_in, conv_w, conv_b, g_conv):
    a = glu_in[..., :CONV_CH] * jax.nn.sigmoid(glu_in[..., CONV_CH:])
    y = lax.conv_general_dilated(
        a, conv_w[:, None, :].astype(a.dtype), window_strides=(1,),
        padding=[(CONV_K - 1, 0)], dimension_numbers=('NWC', 'WIO', 'NWC'),
        feature_group_count=CONV_CH) + conv_b
    return jax.nn.silu(rmsnorm(y, g_conv))


def hybrid_mixer(u, w_in, b_f, w_attn_out, conv_w, conv_b, g_conv, w_conv_out, w_out):
    b, s, _ = u.shape
    z = u @ w_in
    q = z[..., Q0:K0].reshape(b, s, N_HEADS, HEAD_DIM)
    k = z[..., K0:V0].reshape(b, s, N_HEADS, HEAD_DIM)
    v = z[..., V0:F0].reshape(b, s, N_HEADS, HEAD_DIM)
    f_logit = z[..., F0:C0] + b_f
    y_attn = forgetting_attention(q, k, v, f_logit) @ w_attn_out
    y_conv = conformer_conv(z[..., C0:GA0], conv_w, conv_b, g_conv) @ w_conv_out
    merged = jax.nn.sigmoid(z[..., GA0:GC0]) * y_attn + jax.nn.sigmoid(z[..., GC0:]) * y_conv
    return merged @ w_out


def setup_inputs(seed: int = 0) -> dict:
    key = jax.random.key(seed)
    ks = jax.random.split(key, 24)
    L, D, F = DEPTH, D_MODEL, D_FF
    f32 = jnp.float32

    def w(k, shape, fan_in):
        return jax.random.normal(k, shape, f32) * (fan_in ** -0.5)

    def gain(k, shape):
        return 1.0 + 0.05 * jax.random.normal(k, shape, f32)

    return {
        "x": jax.random.normal(ks[0], (BATCH, SEQ, D), f32),
        "p": jax.random.normal(ks[1], (DEPTH, BATCH, SEQ, D_PLE), f32),
        "g_ff1": gain(ks[2], (L, D)),
        "w_ff1_in": w(ks[3], (L, D, 2 * F), D),
        "w_ff1_out": w(ks[4], (L, F, D), F),
        "g_mix": gain(ks[5], (L, D)),
        "w_in": w(ks[6], (L, D, IN_COLS), D),
        "b_f": 2.0 + 0.5 * jax.random.normal(ks[7], (L, N_HEADS), f32),
        "w_attn_out": w(ks[8], (L, ATTN_WIDTH, D), ATTN_WIDTH),
        "conv_w": w(ks[9], (L, CONV_K, CONV_CH), CONV_K),
        "conv_b": 0.02 * jax.random.normal(ks[10], (L, CONV_CH), f32),
        "g_conv": gain(ks[11], (L, CONV_CH)),
        "w_conv_out": w(ks[12], (L, CONV_CH, D), CONV_CH),
        "w_out": w(ks[13], (L, D, D), D),
        "g_ff2": gain(ks[14], (L, D)),
        "w_ff2_in": w(ks[15], (L, D, 2 * F), D),
        "w_ff2_out": w(ks[16], (L, F, D), F),
        "g_ple": gain(ks[17], (L, D)),
        "w_ple_gate": w(ks[18], (L, D, D), D),
        "w_ple_proj": w(ks[19], (L, D_PLE, D), D_PLE),
        "g_final": gain(ks[20], (D,)),
    }


def reference(x, p, g_ff1, w_ff1_in, w_ff1_out, g_mix, w_in, b_f, w_attn_out,
              conv_w, conv_b, g_conv, w_conv_out, w_out, g_ff2, w_ff2_in, w_ff2_out,
              g_ple, w_ple_gate, w_ple_proj, g_final):
    h = x
    for i in range(DEPTH):
        h = h + FFN_RES * swiglu(rmsnorm(h, g_ff1[i]), w_ff1_in[i], w_ff1_out[i])
        h = h + hybrid_mixer(rmsnorm(h, g_mix[i]), w_in[i], b_f[i], w_attn_out[i],
                             conv_w[i], conv_b[i], g_conv[i], w_conv_out[i], w_out[i])
        h = h + FFN_RES * swiglu(rmsnorm(h, g_ff2[i]), w_ff2_in[i], w_ff2_out[i])
        gate = jax.nn.sigmoid(rmsnorm(h, g_ple[i]) @ w_ple_gate[i])
        h = h + gate * (p[i] @ w_ple_proj[i])
    return rmsnorm(h, g_final)
```

```python
import os
import numpy as np
from contextlib import ExitStack
import concourse.bass as bass
import concourse.mybir as mybir
from concourse.bass_utils import run_bass_kernel_spmd

F32 = mybir.dt.float32
BF16 = mybir.dt.bfloat16
AF = mybir.ActivationFunctionType
ALU = mybir.AluOpType

D = 1024; S = 2048; L = 4; NH = 8; DH = 64; DFF = 2816; NJ = 22; DPLE = 256; CK = 31; CC = 512
Q0 = 0; K0 = 512; V0 = 1024; F0 = 1536; C0 = 1544; GA0 = 2568; GC0 = 3592
EPS = 1e-6
TS = 512
NDMA = 8
NS = 3
NB = 4
SM_L = 32 + 8 + 124 + 64
SM_TOT = SM_L * L + 8


class Op:
    __slots__ = ("eng", "fn", "deps", "sig", "needed", "dma", "dma_slot", "phase")


class Prog:
    def __init__(self, nc, es):
        self.nc = nc; self.es = es; self.ops = []
        self.last_w = {}; self.readers = {}
        self.phase = 0
        self.dma_last = [None] * NDMA
        self.dma_rr = 0

    def op(self, eng, fn, r=(), w=(), dma=False):
        i = len(self.ops)
        deps = set()
        rk0 = ("dma", i) if dma else eng
        for k in r:
            if k in self.last_w:
                deps.add(self.last_w[k])
            if isinstance(k, tuple) and k[0] == "ps":
                for e2, idx in self.readers.get(k, {}).items():
                    if e2 != rk0:
                        deps.add(idx)
        for k in w:
            if k in self.last_w:
                deps.add(self.last_w[k])
            deps.update(self.readers.get(k, {}).values())
        rk = ("dma", i) if dma else eng
        for k in r:
            self.readers.setdefault(k, {})[rk] = i
        for k in w:
            self.last_w[k] = i
            self.readers[k] = {}
        o = Op(); o.eng = eng; o.fn = fn; o.dma = dma; o.phase = self.phase; o.dma_slot = -1
        if dma:
            slot = self.dma_rr % NDMA; self.dma_rr += 1
            if self.dma_last[slot] is not None:
                deps.add(self.dma_last[slot])
            self.dma_last[slot] = i
            o.dma_slot = slot
        o.deps = [d for d in deps if not (self.ops[d].eng == "pe" and eng == "pe")]
        self.ops.append(o)
        return i

    def emit(self):
        nc = self.nc; ops = self.ops
        for o in ops:
            o.needed = False
        for o in ops:
            for d in o.deps:
                ops[d].needed = True
        sems = []
        semmap = {}
        cnt = {}
        dsem = []
        for i in range(NDMA):
            sems.append(self.es.enter_context(nc.semaphore(f"dma{i}")))
            dsem.append(len(sems) - 1)
        dcnt = [0] * NDMA
        for o in ops:
            if o.dma:
                dcnt[o.dma_slot] += 16
                o.sig = (dsem[o.dma_slot], dcnt[o.dma_slot])
            elif o.needed:
                key = (o.eng, o.phase)
                if key not in semmap:
                    sems.append(self.es.enter_context(nc.semaphore(f"s_{o.eng}_{o.phase}")))
                    semmap[key] = len(sems) - 1
                cnt[key] = cnt.get(key, 0) + 1
                o.sig = (semmap[key], cnt[key])
            else:
                o.sig = None
        with nc.Block() as block:
            def make(eng):
                def body(e):
                    known = {}
                    for o in ops:
                        if o.eng != eng:
                            continue
                        need = {}
                        for d in o.deps:
                            si, val = ops[d].sig
                            if known.get(si, 0) >= val:
                                continue
                            if need.get(si, 0) < val:
                                need[si] = val
                        for si, val in need.items():
                            e.wait_ge(sems[si], val)
                            known[si] = val
                        if o.fn is not None:
                            ins = o.fn(e)
                            if o.sig is not None:
                                ins.then_inc(sems[o.sig[0]], 16 if o.dma else 1)
                return body
            block.sync(make("sp"))
            block.tensor(make("pe"))
            block.scalar(make("act"))
            block.vector(make("dve"))
            block.gpsimd(make("pool"))


def layer_tile_specs():
    specs = []
    for nm in ("w_ff1_in", "w_ff2_in"):
        for j in range(NJ):
            specs.append((nm, 0, 8, j * 128, 128))
            specs.append((nm, 0, 8, DFF + j * 128, 128))
    for nm in ("w_ff1_out", "w_ff2_out"):
        for m in range(8):
            for (k0, kc) in ((0, 8), (8, 8), (16, 6)):
                specs.append((nm, k0, kc, m * 128, 128))
    for c0 in range(0, F0, 128):
        specs.append(("w_in", 0, 8, c0, 128))
    specs.append(("w_in", 0, 8, F0, 8))
    for c0 in range(C0, D * 2 + GA0, 128):
        specs.append(("w_in", 0, 8, c0, 128))
    for m in range(8):
        specs.append(("w_attn_out", 0, 4, m * 128, 128))
        specs.append(("w_conv_out", 0, 4, m * 128, 128))
        specs.append(("w_out", 0, 8, m * 128, 128))
        specs.append(("w_ple_gate", 0, 8, m * 128, 128))
        specs.append(("w_ple_proj", 0, 2, m * 128, 128))
    offs = {}
    off = 0
    for sp in specs:
        offs[sp] = off
        off += sp[2] * sp[4]
    return specs, offs, off


def build_program(depth):
    specs, offs, TPP = layer_tile_specs()
    nc = bass.Bass("TRN2", target_bir_lowering=False)
    xT_d = nc.dram_tensor("xT", [D, S], F32, kind="ExternalInput").ap()
    pT_d = nc.dram_tensor("pT", [L, 2, 128, 2, 1024], F32, kind="ExternalInput").ap()
    wts_d = nc.dram_tensor("wts", [max(depth, 1), 128, TPP], F32, kind="ExternalInput").ap()
    small_d = nc.dram_tensor("small", [128, SM_TOT], F32, kind="ExternalInput").ap()
    consts_d = nc.dram_tensor("consts", [128, 512], F32, kind="ExternalInput").ap()
    ind_d = nc.dram_tensor("ind", [128, 2048], BF16, kind="ExternalInput").ap()
    maskh_d = nc.dram_tensor("maskh", [128, 1024], BF16, kind="ExternalInput").ap()
    outT_d = nc.dram_tensor("outT", [D, S], F32, kind="ExternalOutput").ap()

    es = ExitStack()
    with es:
        def sb(name, shape, dt):
            return es.enter_context(nc.sbuf_tensor("mk_" + name, shape, dt))
        h = sb("h_sb", [128, 8, S], F32)
        xn = sb("xn", [128, 8, 1024], BF16)
        RB = sb("RB", [128, 28672], BF16)
        G = sb("G", [128, 4, 30 + S], BF16)
        stg = [sb(f"stg{i}", [128, 1024], F32) for i in range(NS)]
        wb = [sb(f"wb{i}", [128, 1024], BF16) for i in range(NB)]
        SQ = sb("SQ", [128, 4096], BF16)
        tmp = [sb(f"tmp{i}", [128, 512], F32) for i in range(4)]
        ptl = [sb(f"pt{i}", [128, 512], BF16) for i in range(4)]
        qzb = [[sb(f"qz{a}{b}", [128, 512], BF16) for b in range(1)] for a in range(2)]
        ftok = sb("ftok", [128, 128], F32)
        sp_t = sb("sp_t", [128, 128], F32)
        tot = sb("tot", [128, 128], F32)
        pre = sb("pre", [128, 128], F32)
        cstok = sb("cstok", [128, 128], F32)
        refb = sb("refb", [128, 128], F32)
        cst = sb("cst", [128, 512], F32)
        tri_b = sb("tri_b", [128, 128], BF16)
        ones_b = sb("ones_b", [128, 128], BF16)
        ident_b = sb("ident_b", [128, 128], BF16)
        small = sb("small_sb", [128, SM_TOT], F32)
        dummy = sb("dummy", [128, 2], F32)
        ps = [es.enter_context(nc.psum_tensor(f"ps{i}", [128, 512], F32)) for i in range(8)]

        pbuf = [sb(f"pbuf{i}", [128, 1024], BF16) for i in range(2)]
        qT = RB[:, 0:8192].rearrange("p (c t) -> p c t", c=4)
        kT = RB[:, 8192:16384].rearrange("p (c t) -> p c t", c=4)
        vaug = RB[:, 16384:28672].rearrange("p (i c f) -> p i c f", i=16, c=4)
        hid = RB[:, 0:22528].rearrange("p (j t) -> p j t", j=NJ)
        mrg = RB[:, 8192:16384].rearrange("p (k t) -> p k t", k=8)
        diag = RB[:, 8192:8192 + 4 * CK * 128].rearrange("p (c k m) -> p c k m", c=4, k=CK)
        sq = SQ[:, :].rearrange("p (k t) -> p k t", k=8)
        bias = SQ[:, 0:1024].bitcast(F32).rearrange("p (j i h) -> p j i h", j=4, i=16)
        xnf = xn[:, :, :].rearrange("p k t -> p (k t)")
        INDv = xnf[:, 0:2048]
        MASKHv = xnf[:, 2048:3072].rearrange("p (h s) -> p h s", h=8)
        AUGL = xnf[:, 3072:4096].rearrange("p (h s) -> p h s", h=8)
        XNKEYS = [("xn", k, tl) for k in range(8) for tl in range(2)]
        ident_f = cst[:, 0:128]; tri_f = cst[:, 128:256]; e64_f = cst[:, 256:384]; ones_f = cst[:, 384:512]
        RALL = ["RQu", "RKu", "RVu"]

        P = Prog(nc, es)
        state = {"sn": 0, "bn": 0, "ps": 0, "tmp": 0, "pt": 0, "xb": 0, "sb": 0, "qz": [0, 0]}

        def mm(out, lhsT, rhs, start, stop, r, w):
            P.op("pe", lambda e: e.matmul(out, lhsT, rhs, start=start, stop=stop), r=r, w=w)

        def act(out, in_, func, r, w, **kw):
            P.op("act", lambda e: e.activation(out=out, in_=in_, func=func, **kw), r=r, w=w)

        def acopy(out, in_, r, w):
            P.op("act", lambda e: e.copy(out=out, in_=in_), r=r, w=w)

        def vcopy(out, in_, r, w):
            P.op("dve", lambda e: e.tensor_copy(out=out, in_=in_), r=r, w=w)

        def vtt(out, in0, in1, op, r, w):
            P.op("dve", lambda e: e.tensor_tensor(out=out, in0=in0, in1=in1, op=op), r=r, w=w)

        def vstt(out, in0, scalar, in1, op0, op1, r, w):
            P.op("dve", lambda e: e.scalar_tensor_tensor(out=out, in0=in0, scalar=scalar, in1=in1, op0=op0, op1=op1), r=r, w=w)

        def vts(out, in0, scalar1, op0, r, w):
            P.op("dve", lambda e: e.tensor_scalar(out=out, in0=in0, scalar1=scalar1, scalar2=None, op0=op0), r=r, w=w)

        def vmemset(ap, val, r, w):
            P.op("dve", lambda e: e.memset(ap, val), r=r, w=w)

        def vrecip(out, in_, r, w):
            P.op("dve", lambda e: e.reciprocal(out=out, in_=in_), r=r, w=w)

        def dma(out, in_, r, w):
            return P.op("sp", lambda e: e.dma_start(out=out, in_=in_), r=r, w=w, dma=True)

        def pcast(out, in_, r, w):
            P.op("pool", lambda e: e.tensor_copy(out=out, in_=in_), r=r, w=w)

        def fence(keys):
            vmemset(dummy[:, 0:1], 0.0, [], list(keys) + ["dummy"])

        def psb():
            i = state["ps"] % 8; state["ps"] += 1
            return i

        def tmpi():
            i = state["tmp"] % 4; state["tmp"] += 1
            return i

        def stage(src, ne):
            si = state["sn"] % NS; state["sn"] += 1
            dma(stg[si][:, 0:ne], src, [], [("stg", si)])
            return si

        def wtile(l, spec):
            _, k0, kc, c0, ncol = spec
            ne = kc * ncol
            si = stage(wts_d[l, :, offs[spec]:offs[spec] + ne], ne)
            bi = state["bn"] % NB; state["bn"] += 1
            pcast(wb[bi][:, 0:ne], stg[si][:, 0:ne], [("stg", si)], [("wb", bi)])
            return wb[bi][:, 0:ne].rearrange("p (k c) -> p k c", k=kc), ("wb", bi)

        def sm(l, off, n=1):
            return small[:, l * SM_L + off: l * SM_L + off + n]

        def tsl(t):
            return slice(t * TS, (t + 1) * TS)

        dma(small[:, :], small_d[:, :], [], ["small"])
        dma(cst[:, :], consts_d[:, :], [], ["cst"])
        for hh0 in range(2):
            for c in range(8):
                dma(h[:, c, hh0 * 1024:(hh0 + 1) * 1024], xT_d[c * 128:(c + 1) * 128, hh0 * 1024:(hh0 + 1) * 1024], [],
                    [("h", c, 2 * hh0), ("h", c, 2 * hh0 + 1)])
        P.op("dve", lambda e: e.tensor_scalar(out=tri_b[:, :], in0=tri_f, scalar1=-1.0, scalar2=30000.0, op0=ALU.add, op1=ALU.mult),
             r=["cst"], w=["tri_b"])
        vcopy(ident_b[:, :], ident_f, ["cst"], ["ident_b"])
        vcopy(ones_b[:, :], ones_f, ["cst"], ["ones_b"])
        vmemset(G[:, :, 0:30], 0.0, [], ["Gpad"])
        for a in range(2):
            for b in range(1):
                vmemset(qzb[a][b][(1 - a) * 64:(2 - a) * 64, :], 0.0, [], [("qz", a, b)])

        def rstd_tile(src_ap, rkeys, nk, inv_n, sq_kw=None):
            if sq_kw is None:
                act(sq[:, 0:nk, :], src_ap, AF.Square, list(rkeys) + ["SQu"], ["sq"])
            b = psb()
            for k in range(nk):
                mm(ps[b][:, :], ones_b[:, :], sq[:, k, :], k == 0, k == nk - 1, ["ones_b", "sq", "SQu"], [("ps", b)])
            ti = tmpi()
            act(tmp[ti][:, :], ps[b][:, :], AF.Ln, [("ps", b)], [("tmp", ti)], scale=inv_n, bias=EPS)
            act(tmp[ti][:, :], tmp[ti][:, :], AF.Exp, [("tmp", ti)], [("tmp", ti)], scale=-0.5)
            return ti

        def norm_half(l, hh, goff):
            for tl in range(2):
                t = 2 * hh + tl
                ti = rstd_tile(h[:, :, tsl(t)], [("h", k, t) for k in range(8)], 8, 1.0 / D)
                for k in range(8):
                    vstt(xn[:, k, tsl(tl)], h[:, k, tsl(t)], sm(l, goff + k), tmp[ti][:, :], ALU.mult, ALU.mult,
                         [("h", k, t), ("tmp", ti), "small"], [("xn", k, tl)])

        def proj(wt, kw, nk, rhs_fn, rkeys_fn, b):
            for k in range(nk):
                mm(ps[b][:, :], wt[:, k, :], rhs_fn(k), k == 0, k == nk - 1, [kw] + rkeys_fn(k), [("ps", b)])

        def ffn(l, w_in_nm, w_out_nm, goff, pre0=False, hook=None):
            fence(RALL)
            for hh in range(2):
                if hh == 0 and not pre0:
                    norm_half(l, 0, goff)
                for j in range(NJ):
                    wa, ka = wtile(l, (w_in_nm, 0, 8, j * 128, 128))
                    wg, kg = wtile(l, (w_in_nm, 0, 8, DFF + j * 128, 128))
                    for tl in range(2):
                        ba = psb(); bb = psb()
                        proj(wa, ka, 8, lambda k: xn[:, k, tsl(tl)], lambda k: [("xn", k, tl)], ba)
                        proj(wg, kg, 8, lambda k: xn[:, k, tsl(tl)], lambda k: [("xn", k, tl)], bb)
                        ti = tmpi()
                        act(tmp[ti][:, :], ps[ba][:, :], AF.Silu, [("ps", ba)], [("tmp", ti)])
                        vtt(hid[:, j, tsl(tl)], ps[bb][:, :], tmp[ti][:, :], ALU.mult,
                            [("ps", bb), ("tmp", ti)] + RALL, [("hid", j, tl)])
                for m in range(8):
                    if hh == 0 and m == 3:
                        norm_half(l, 1, goff)
                    if hh == 1 and m == 3 and hook is not None:
                        hook()
                    banks = [psb(), psb()]
                    for (k0, kc) in ((0, 8), (8, 8), (16, 6)):
                        wt, kw = wtile(l, (w_out_nm, k0, kc, m * 128, 128))
                        for tl in range(2):
                            for kk in range(kc):
                                j = k0 + kk
                                mm(ps[banks[tl]][:, :], wt[:, kk, :], hid[:, j, tsl(tl)], j == 0, j == NJ - 1,
                                   [kw, ("hid", j, tl)] + RALL, [("ps", banks[tl])])
                    for tl in range(2):
                        t = 2 * hh + tl
                        vstt(h[:, m, tsl(t)], ps[banks[tl]][:, :], 0.5, h[:, m, tsl(t)], ALU.mult, ALU.add,
                             [("ps", banks[tl]), ("h", m, t)], [("h", m, t)])

        def mixer(l, pre0=False, hook=None):
            fence(RALL + ["SQu"])
            gmix = 8
            vmemset(vaug[:, :, :, 64:128], 1.0, ["RVu"], ["vones"])
            only = os.environ.get("MK_ONLY", "qkvfg")
            for hh in range(2):
                if not (hh == 0 and pre0):
                    norm_half(l, hh, gmix)
                for c in range(4 if "q" in only else 0):
                    wt, kw = wtile(l, ("w_in", 0, 8, Q0 + c * 128, 128))
                    for tl in range(2):
                        t = 2 * hh + tl
                        b = psb()
                        proj(wt, kw, 8, lambda k: xn[:, k, tsl(tl)], lambda k: [("xn", k, tl)], b)
                        acopy(qT[:, c, tsl(t)], ps[b][:, :], [("ps", b), "RQu"], [("q", c, t, 0), ("q", c, t, 1)])
                for c in range(4 if "k" in only else 0):
                    wt, kw = wtile(l, ("w_in", 0, 8, K0 + c * 128, 128))
                    for tl in range(2):
                        t = 2 * hh + tl
                        b = psb()
                        proj(wt, kw, 8, lambda k: xn[:, k, tsl(tl)], lambda k: [("xn", k, tl)], b)
                        vcopy(kT[:, c, tsl(t)], ps[b][:, :], [("ps", b), "RKu"], [("k", c, 4 * t + i) for i in range(4)])
                for c in range(4 if "v" in only else 0):
                    wt, kw = wtile(l, ("w_in", 0, 8, V0 + c * 128, 128))
                    for il in range(8):
                        i = 8 * hh + il
                        tl = il // 4
                        b = psb()
                        for k in range(8):
                            mm(ps[b][:, 0:128], xn[:, k, il * 128:(il + 1) * 128], wt[:, k, :], k == 0, k == 7,
                               [kw, ("xn", k, tl)], [("ps", b)])
                        if il % 2 == 0:
                            vcopy(vaug[:, i, c, 0:64], ps[b][:, 0:64], [("ps", b), "RVu"], [("v", i, c, 0)])
                            vcopy(vaug[:, i, c, 128:192], ps[b][:, 64:128], [("ps", b), "RVu"], [("v", i, c, 1)])
                        else:
                            acopy(vaug[:, i, c, 0:64], ps[b][:, 0:64], [("ps", b), "RVu"], [("v", i, c, 0)])
                            acopy(vaug[:, i, c, 128:192], ps[b][:, 64:128], [("ps", b), "RVu"], [("v", i, c, 1)])
                wt, kw = wtile(l, ("w_in", 0, 8, F0, 8))
                b = psb()
                for il in range(8 if "f" in only else 0):
                    tl = il // 4
                    for k in range(8):
                        mm(ps[b][:, il * 8:(il + 1) * 8], xn[:, k, il * 128:(il + 1) * 128], wt[:, k, :], k == 0, k == 7,
                           [kw, ("xn", k, tl)], [("ps", b)])
                if "f" in only:
                    vtt(ftok[:, hh * 64:(hh + 1) * 64], ps[b][:, 0:64], sm(l, 164, 64), ALU.add,
                        [("ps", b), "small"], [("ftok", hh)])
                for c in range(4 if "g" in only else 0):
                    wa, ka = wtile(l, ("w_in", 0, 8, C0 + c * 128, 128))
                    wg, kg = wtile(l, ("w_in", 0, 8, C0 + CC + c * 128, 128))
                    for tl in range(2):
                        t = 2 * hh + tl
                        ba = psb(); bb = psb()
                        proj(wa, ka, 8, lambda k: xn[:, k, tsl(tl)], lambda k: [("xn", k, tl)], ba)
                        proj(wg, kg, 8, lambda k: xn[:, k, tsl(tl)], lambda k: [("xn", k, tl)], bb)
                        ti = tmpi()
                        act(tmp[ti][:, :], ps[bb][:, :], AF.Sigmoid, [("ps", bb)], [("tmp", ti)])
                        vtt(G[:, c, 30 + t * TS:30 + (t + 1) * TS], ps[ba][:, :], tmp[ti][:, :], ALU.mult,
                            [("ps", ba), ("tmp", ti)], [("G", c, t)])
            mixlvl = int(os.environ.get("MK_MIX", "9"))
            if mixlvl < 2:
                return
            act(sp_t[:, :], ftok[:, :], AF.Exp, [("ftok", 0), ("ftok", 1)], ["sp_t"], scale=-1.0)
            act(sp_t[:, :], sp_t[:, :], AF.Ln, ["sp_t"], ["sp_t"], scale=1.0, bias=1.0)
            b1 = psb(); b2 = psb()
            mm(ps[b1][:, 0:128], tri_f, sp_t[:, :], True, True, ["cst", "sp_t"], [("ps", b1)])
            mm(ps[b2][:, 0:128], ones_f, sp_t[:, :], True, True, ["cst", "sp_t"], [("ps", b2)])
            vcopy(tot[:, :], ps[b2][:, 0:128], [("ps", b2)], ["tot"])
            tot3 = tot[:, :].rearrange("p (i h) -> p i h", i=16)
            pre3 = pre[:, :].rearrange("p (i h) -> p i h", i=16)
            for hd in range(NH):
                P.op("dve", lambda e, hd=hd: e.tensor_tensor_scan(out=pre3[:, :, hd], data0=ones_f[:, 0:16], data1=tot3[:, :, hd],
                                                                  initial=0.0, op0=ALU.mult, op1=ALU.add),
                     r=["tot", "cst", "pre"], w=[("pre", hd)])
            vtt(pre[:, :], pre[:, :], tot[:, :], ALU.subtract, [("pre", hd) for hd in range(NH)] + ["tot"],
                ["pre"] + [("pre", hd) for hd in range(NH)])
            vtt(cstok[:, :], ps[b1][:, 0:128], pre[:, :], ALU.add, [("ps", b1), "pre"], ["cstok"])
            b3 = psb()
            mm(ps[b3][:, 0:128], e64_f, cstok[:, :], True, True, ["cst", "cstok"], [("ps", b3)])
            vcopy(refb[:, :], ps[b3][:, 0:128], [("ps", b3)], ["refb"])
            for bq in range(16):
                jr = 4 * (bq // 4) + 2
                vtt(tot[:, bq * 8:(bq + 1) * 8], refb[:, bq * 8:(bq + 1) * 8], refb[:, jr * 8:(jr + 1) * 8], ALU.subtract,
                    ["refb", "tot"], ["tot"])
            fence(XNKEYS + ["xnalias"])
            dma(INDv, ind_d[:, :], ["xnalias"], ["ind"])
            dma(MASKHv.rearrange("p h s -> p (h s)"), maskh_d[:, :], ["xnalias"], ["maskh"])
            bt = psb()
            P.op("pe", lambda e: e.transpose(ps[bt][:, 0:128], tot[:, :], ident_f), r=["tot", "cst"], w=[("ps", bt)])
            for hd in range(NH):
                P.op("dve", lambda e, hd=hd: e.tensor_scalar(out=AUGL[:, hd, :], in0=MASKHv[:, hd, :], scalar1=ps[bt][:, 0:1],
                                                             scalar2=-8.0, op0=ALU.mult, op1=ALU.mult),
                     r=[("ps", bt), "maskh", "xnalias"], w=[("augl", hd)])
            fence(["SQu"])
            cs3 = cstok[:, :].rearrange("p (i h) -> p i h", i=16)
            for j in range(4):
                for hd in range(NH):
                    col = (4 * j + 2) * 8 + hd
                    vts(bias[:, j, 0:4 * j + 4, hd], cs3[:, 0:4 * j + 4, hd], refb[:, col:col + 1], ALU.subtract,
                        ["cstok", "refb", "SQu"], [("bias", j)])
            if mixlvl < 3:
                return
            units = [(c, j, hx) for c in range(4) for j in range(4) for hx in range(2)]

            def qz_copy(u):
                c_, j_, hx_ = u
                pr = slice(hx_ * 64, (hx_ + 1) * 64)
                pcast(qzb[hx_][0][pr, :], qT[pr, c_, tsl(j_)], [("q", c_, j_, hx_), "RQu"], [("qz", hx_, 0)])
            qz_copy(units[0])
            for ui, (c, j, hx) in enumerate(units):
                if ui + 1 < len(units):
                    qz_copy(units[ui + 1])
                attn_unit(c, j, hx)
            fence(XNKEYS + ["xnalias", "SQu"])
            if mixlvl < 4:
                return
            norm_half(l, 0, gmix)
            fence(["RKu", "RVu"])
            for c in range(4):
                for k in range(CK):
                    if k % 2 == 0:
                        vts(diag[:, c, k, :], ident_f, sm(l, 40 + c * CK + k), ALU.mult,
                            ["cst", "small", "RKu", "RVu"], [("diag", c)])
                    else:
                        act(diag[:, c, k, :], ident_f, AF.Identity, ["cst", "small", "RKu", "RVu"], [("diag", c)],
                            scale=sm(l, 40 + c * CK + k))
            for t in (3, 2, 1, 0):
                cb = [psb() for _ in range(4)]
                for c in range(4):
                    rk = [("G", c, t)] + ([("G", c, t - 1)] if t > 0 else ["Gpad"])
                    for k in range(CK):
                        mm(ps[cb[c]][:, :], diag[:, c, k, :], G[:, c, t * TS + k:t * TS + k + TS], k == 0, k == CK - 1,
                           [("diag", c), "RKu", "RVu"] + rk, [("ps", cb[c])])
                for c in range(4):
                    act(sq[:, c, :], ps[cb[c]][:, :], AF.Square, [("ps", cb[c]), "small", "SQu"], ["sq"],
                        bias=sm(l, 32 + c), scale=1.0)
                ti = rstd_tile(None, [], 4, 1.0 / CC, sq_kw=True)
                for c in range(4):
                    t2 = tmpi()
                    vstt(tmp[t2][:, :], ps[cb[c]][:, :], sm(l, 32 + c), tmp[ti][:, :], ALU.add, ALU.mult,
                         [("ps", cb[c]), ("tmp", ti), "small"], [("tmp", t2)])
                    act(G[:, c, 30 + t * TS:30 + (t + 1) * TS], tmp[t2][:, :], AF.Silu, [("tmp", t2), "small"], [("G", c, t)],
                        scale=sm(l, 36 + c))
            if mixlvl < 5:
                return
            fence(["RKu", "RVu"])
            for hh in range(2):
                for m in range(8):
                    tt_ = []
                    for (wnm, gc0, src_is_attn) in (("w_attn_out", GA0, True), ("w_conv_out", GC0, False)):
                        wy, ky = wtile(l, (wnm, 0, 4, m * 128, 128))
                        wgt, kgt = wtile(l, ("w_in", 0, 8, gc0 + m * 128, 128))
                        row = []
                        for tl in range(2):
                            t = 2 * hh + tl
                            by = psb(); bg = psb()
                            if src_is_attn:
                                proj(wy, ky, 4, lambda k: qT[:, k, tsl(t)], lambda k: [("q", k, t, 0), ("q", k, t, 1), "RQu"], by)
                            else:
                                proj(wy, ky, 4, lambda k: G[:, k, 30 + t * TS:30 + (t + 1) * TS], lambda k: [("G", k, t)], by)
                            proj(wgt, kgt, 8, lambda k: xn[:, k, tsl(tl)], lambda k: [("xn", k, tl)], bg)
                            t1 = tmpi()
                            act(tmp[t1][:, :], ps[bg][:, :], AF.Sigmoid, [("ps", bg)], [("tmp", t1)])
                            if src_is_attn:
                                vtt(mrg[:, m, tsl(tl)], ps[by][:, :], tmp[t1][:, :], ALU.mult,
                                    [("ps", by), ("tmp", t1), "RKu"], [("mrgA", m, tl), ("mrg", m, tl)])
                            else:
                                vtt(tmp[t1][:, :], ps[by][:, :], tmp[t1][:, :], ALU.mult,
                                    [("ps", by), ("tmp", t1)], [("tmp", t1)])
                                vtt(mrg[:, m, tsl(tl)], mrg[:, m, tsl(tl)], tmp[t1][:, :], ALU.add,
                                    [("mrgA", m, tl), ("tmp", t1), "RKu"], [("mrg", m, tl), ("mrgA", m, tl)])
                for m in range(8):
                    if hh == 0 and m == 2:
                        norm_half(l, 1, gmix)
                    if hh == 1 and m == 2 and hook is not None:
                        hook()
                    wo, ko = wtile(l, ("w_out", 0, 8, m * 128, 128))
                    for tl in range(2):
                        t = 2 * hh + tl
                        b = psb()
                        proj(wo, ko, 8, lambda k: mrg[:, k, tsl(tl)], lambda k: [("mrg", k, tl), "RKu"], b)
                        vtt(h[:, m, tsl(t)], ps[b][:, :], h[:, m, tsl(t)], ALU.add, [("ps", b), ("h", m, t)], [("h", m, t)])

        def attn_unit(c, j, hx):
            hd = 2 * c + hx
            prow = slice(hx * 64, (hx + 1) * 64)
            drow = slice((1 - hx) * 64, (2 - hx) * 64)
            lo = 0 if hx == 0 else 64
            xb = 6 + (state["xb"] % 2); state["xb"] += 1
            zi = 0
            qz = qzb[hx][zi]
            nki = 4 * j + 4
            pend = []

            def qk(i):
                q0 = max(0, i - 4 * j)
                sbk = state["sb"] % 6; state["sb"] += 1
                mm(ps[sbk][:, q0 * 128:512], kT[:, c, i * 128:(i + 1) * 128],
                   qz[:, q0 * 128:512], True, False,
                   [("k", c, i), ("qz", hx, zi), "RKu"], [("ps", sbk)])
                diag_blk = i >= 4 * j
                mm(ps[sbk][:, q0 * 128:512], AUGL[:, hd, :], INDv[:, j * TS + q0 * 128:(j + 1) * TS], False, not diag_blk,
                   [("augl", hd), "ind", "xnalias"], [("ps", sbk)])
                if diag_blk:
                    mm(ps[sbk][:, q0 * 128:(q0 + 1) * 128], ident_b[:, :], tri_b[:, :], False, True,
                       ["ident_b", "tri_b"], [("ps", sbk)])
                pi = state["pt"] % 4; state["pt"] += 1
                act(ptl[pi][:, q0 * 128:512], ps[sbk][:, q0 * 128:512], AF.Exp,
                    [("ps", sbk), ("bias", j), "SQu"], [("pt", pi)],
                    scale=0.125, bias=bias[:, j, i, hd:hd + 1])
                return (i, q0, pi)

            def pv(item):
                i, q0, pi = item
                mm(ps[xb][:, q0 * 128:512], vaug[:, i, c, lo:lo + 128], ptl[pi][:, q0 * 128:512],
                   i == 0, i == nki - 1, [("v", i, c, hx), "vones", ("pt", pi), "RVu"], [("ps", xb)])

            for i in range(nki):
                pend.append(qk(i))
                if len(pend) > 3:
                    pv(pend.pop(0))
            while pend:
                pv(pend.pop(0))
            ti = tmpi()
            vrecip(tmp[ti][prow, :], ps[xb][drow, :], [("ps", xb)], [("tmp", ti)])
            vtt(qT[prow, c, tsl(j)], ps[xb][prow, :], tmp[ti][prow, :], ALU.mult,
                [("ps", xb), ("tmp", ti), "RQu"], [("q", c, j, hx)])

        def ple(l, pre0=False):
            for hh in range(2):
                if not (hh == 0 and pre0):
                    norm_half(l, hh, 24)
                for k in range(2):
                    si = stage(pT_d[l, hh, :, k, :], 1024)
                    pcast(pbuf[k][:, :], stg[si][:, 0:1024], [("stg", si)], [("pbuf", k)])
                for m in range(8):
                    wg, kg = wtile(l, ("w_ple_gate", 0, 8, m * 128, 128))
                    wp, kp = wtile(l, ("w_ple_proj", 0, 2, m * 128, 128))
                    for tl in range(2):
                        t = 2 * hh + tl
                        bg = psb(); bp = psb()
                        proj(wg, kg, 8, lambda k: xn[:, k, tsl(tl)], lambda k: [("xn", k, tl)], bg)
                        proj(wp, kp, 2, lambda k: pbuf[k][:, tsl(tl)], lambda k: [("pbuf", k)], bp)
                        t1 = tmpi()
                        act(tmp[t1][:, :], ps[bg][:, :], AF.Sigmoid, [("ps", bg)], [("tmp", t1)])
                        vtt(tmp[t1][:, :], ps[bp][:, :], tmp[t1][:, :], ALU.mult, [("ps", bp), ("tmp", t1)], [("tmp", t1)])
                        vtt(h[:, m, tsl(t)], tmp[t1][:, :], h[:, m, tsl(t)], ALU.add, [("tmp", t1), ("h", m, t)], [("h", m, t)])

        for l in range(depth):
            P.phase = l
            parts = os.environ.get("MK_PARTS", "fmgp")
            full = parts == "fmgp"
            if "f" in parts:
                ffn(l, "w_ff1_in", "w_ff1_out", 0, False, (lambda l=l: norm_half(l, 0, 8)) if full else None)
            if "m" in parts:
                mixer(l, full, (lambda l=l: norm_half(l, 0, 16)) if full else None)
            if "g" in parts:
                ffn(l, "w_ff2_in", "w_ff2_out", 16, full, (lambda l=l: norm_half(l, 0, 24)) if full else None)
            if "p" in parts:
                ple(l, full)

        P.phase = depth
        gf = SM_L * L
        for t in range(4):
            ti = rstd_tile(h[:, :, tsl(t)], [("h", k, t) for k in range(8)], 8, 1.0 / D)
            for k in range(8):
                vstt(h[:, k, tsl(t)], h[:, k, tsl(t)], small[:, gf + k:gf + k + 1], tmp[ti][:, :], ALU.mult, ALU.mult,
                     [("h", k, t), ("tmp", ti), "small"], [("h", k, t)])
            dma(outT_d.rearrange("(c p) s -> p c s", p=128)[:, :, tsl(t)], h[:, :, tsl(t)],
                [("h", k, t) for k in range(8)], [("out", t)])
        P.op("sp", None, r=[("out", t) for t in range(4)])
        P.emit()
    return nc


def _pack(inputs, depth):
    specs, offs, TPP = layer_tile_specs()
    wts = np.zeros((max(depth, 1), 128, TPP), np.float32)
    for l in range(depth):
        for sp in specs:
            nm, k0, kc, c0, ncol = sp
            W = inputs[nm][l]
            blk = W[k0 * 128:(k0 + kc) * 128, c0:c0 + ncol].reshape(kc, 128, ncol).transpose(1, 0, 2).reshape(128, kc * ncol)
            wts[l, :, offs[sp]:offs[sp] + kc * ncol] = blk
    small = np.zeros((128, SM_TOT), np.float32)

    def fm(v, nchunk):
        return np.asarray(v, np.float32).reshape(nchunk, 128).T
    for l in range(L):
        o = l * SM_L
        small[:, o + 0:o + 8] = fm(inputs["g_ff1"][l], 8)
        small[:, o + 8:o + 16] = fm(inputs["g_mix"][l], 8)
        small[:, o + 16:o + 24] = fm(inputs["g_ff2"][l], 8)
        small[:, o + 24:o + 32] = fm(inputs["g_ple"][l], 8)
        small[:, o + 32:o + 36] = fm(inputs["conv_b"][l], 4)
        small[:, o + 36:o + 40] = fm(inputs["g_conv"][l], 4)
        cw = np.asarray(inputs["conv_w"][l], np.float32)
        small[:, o + 40:o + 164] = cw.reshape(CK, 4, 128).transpose(2, 1, 0).reshape(128, 4 * CK)
        small[:, o + 164:o + 228] = np.tile(np.asarray(inputs["b_f"][l], np.float32)[None, :], (128, 8))
    small[:, SM_L * L:] = fm(inputs["g_final"], 8)
    consts = np.zeros((128, 512), np.float32)
    consts[:, 0:128] = np.eye(128, dtype=np.float32)
    consts[:, 128:256] = np.triu(np.ones((128, 128), np.float32))
    consts[64, 256:384] = 1.0
    consts[:, 384:512] = 1.0
    return wts, small, consts


def kernel(**inputs):
    depth = int(os.environ.get("MK_DEPTH", L))
    inputs = {k: np.asarray(v) for k, v in inputs.items()}
    wts, small, consts = _pack(inputs, depth)
    x = inputs["x"].astype(np.float32, copy=False)
    p = inputs["p"].astype(np.float32, copy=False)
    nc = build_program(depth)
    import ml_dtypes
    pidx = np.arange(128)
    ind = (pidx[:, None] // 8 == (np.arange(2048)[None, :] // 128)).astype(ml_dtypes.bfloat16)
    maskh = np.repeat((pidx[:, None] % 8 == np.arange(8)[None, :]).astype(ml_dtypes.bfloat16), 128, axis=1)
    in_maps = []
    for b in range(8):
        xT = np.ascontiguousarray(x[b].T)
        pT = np.ascontiguousarray(p[:, b].reshape(L, 2, 1024, 2, 128).transpose(0, 1, 4, 3, 2))
        in_maps.append({"xT": xT, "pT": pT, "wts": wts, "small": small, "consts": consts, "ind": ind, "maskh": maskh})
    res = run_bass_kernel_spmd(nc, in_maps, core_ids=list(range(8)))
    out = np.stack([np.ascontiguousarray(r["outT"].T) for r in res.results], axis=0)
    return out.astype(np.float32, copy=False)
```

```python
import os
import numpy as np
from contextlib import ExitStack
import concourse.bass as bass
import concourse.mybir as mybir
from concourse.bass_utils import run_bass_kernel_spmd

F32 = mybir.dt.float32
BF16 = mybir.dt.bfloat16
AF = mybir.ActivationFunctionType
ALU = mybir.AluOpType

D = 1024; S = 2048; L = 4; NH = 8; DH = 64; DFF = 2816; NJ = 22; DPLE = 256; CK = 31; CC = 512
Q0 = 0; K0 = 512; V0 = 1024; F0 = 1536; C0 = 1544; GA0 = 2568; GC0 = 3592
EPS = 1e-6
TS = 512
NDMA = 8
NS = 3
NB = 4
SM_L = 32 + 8 + 124 + 64
SM_TOT = SM_L * L + 8


class Op:
    __slots__ = ("eng", "fn", "deps", "sig", "needed", "dma", "dma_slot", "phase")


class Prog:
    def __init__(self, nc, es):
        self.nc = nc; self.es = es; self.ops = []
        self.last_w = {}; self.readers = {}
        self.phase = 0
        self.dma_last = [None] * NDMA
        self.dma_rr = 0

    def op(self, eng, fn, r=(), w=(), dma=False):
        i = len(self.ops)
        deps = set()
        rk0 = ("dma", i) if dma else eng
        for k in r:
            if k in self.last_w:
                deps.add(self.last_w[k])
            if isinstance(k, tuple) and k[0] == "ps":
                for e2, idx in self.readers.get(k, {}).items():
                    if e2 != rk0:
                        deps.add(idx)
        for k in w:
            if k in self.last_w:
                deps.add(self.last_w[k])
            deps.update(self.readers.get(k, {}).values())
        rk = ("dma", i) if dma else eng
        for k in r:
            self.readers.setdefault(k, {})[rk] = i
        for k in w:
            self.last_w[k] = i
            self.readers[k] = {}
        o = Op(); o.eng = eng; o.fn = fn; o.dma = dma; o.phase = self.phase; o.dma_slot = -1
        if dma:
            slot = self.dma_rr % NDMA; self.dma_rr += 1
            if self.dma_last[slot] is not None:
                deps.add(self.dma_last[slot])
            self.dma_last[slot] = i
            o.dma_slot = slot
        o.deps = [d for d in deps if not (self.ops[d].eng == "pe" and eng == "pe")]
        self.ops.append(o)
        return i

    def emit(self):
        nc = self.nc; ops = self.ops
        for o in ops:
            o.needed = False
        for o in ops:
            for d in o.deps:
                ops[d].needed = True
        sems = []
        semmap = {}
        cnt = {}
        dsem = []
        for i in range(NDMA):
            sems.append(self.es.enter_context(nc.semaphore(f"dma{i}")))
            dsem.append(len(sems) - 1)
        dcnt = [0] * NDMA
        for o in ops:
            if o.dma:
                dcnt[o.dma_slot] += 16
                o.sig = (dsem[o.dma_slot], dcnt[o.dma_slot])
            elif o.needed:
                key = (o.eng, o.phase)
                if key not in semmap:
                    sems.append(self.es.enter_context(nc.semaphore(f"s_{o.eng}_{o.phase}")))
                    semmap[key] = len(sems) - 1
                cnt[key] = cnt.get(key, 0) + 1
                o.sig = (semmap[key], cnt[key])
            else:
                o.sig = None
        with nc.Block() as block:
            def make(eng):
                def body(e):
                    known = {}
                    for o in ops:
                        if o.eng != eng:
                            continue
                        need = {}
                        for d in o.deps:
                            si, val = ops[d].sig
                            if known.get(si, 0) >= val:
                                continue
                            if need.get(si, 0) < val:
                                need[si] = val
                        for si, val in need.items():
                            e.wait_ge(sems[si], val)
                            known[si] = val
                        if o.fn is not None:
                            ins = o.fn(e)
                            if o.sig is not None:
                                ins.then_inc(sems[o.sig[0]], 16 if o.dma else 1)
                return body
            block.sync(make("sp"))
            block.tensor(make("pe"))
            block.scalar(make("act"))
            block.vector(make("dve"))
            block.gpsimd(make("pool"))


def layer_tile_specs():
    specs = []
    for nm in ("w_ff1_in", "w_ff2_in"):
        for j in range(NJ):
            specs.append((nm, 0, 8, j * 128, 128))
            specs.append((nm, 0, 8, DFF + j * 128, 128))
    for nm in ("w_ff1_out", "w_ff2_out"):
        for m in range(8):
            for (k0, kc) in ((0, 8), (8, 8), (16, 6)):
                specs.append((nm, k0, kc, m * 128, 128))
    for c0 in range(0, F0, 128):
        specs.append(("w_in", 0, 8, c0, 128))
    specs.append(("w_in", 0, 8, F0, 8))
    for c0 in range(C0, D * 2 + GA0, 128):
        specs.append(("w_in", 0, 8, c0, 128))
    for m in range(8):
        specs.append(("w_attn_out", 0, 4, m * 128, 128))
        specs.append(("w_conv_out", 0, 4, m * 128, 128))
        specs.append(("w_out", 0, 8, m * 128, 128))
        specs.append(("w_ple_gate", 0, 8, m * 128, 128))
        specs.append(("w_ple_proj", 0, 2, m * 128, 128))
    offs = {}
    off = 0
    for sp in specs:
        offs[sp] = off
        off += sp[2] * sp[4]
    return specs, offs, off


def build_program(depth):
    specs, offs, TPP = layer_tile_specs()
    nc = bass.Bass("TRN2", target_bir_lowering=False)
    xT_d = nc.dram_tensor("xT", [D, S], F32, kind="ExternalInput").ap()
    pT_d = nc.dram_tensor("pT", [L, 2, 128, 2, 1024], F32, kind="ExternalInput").ap()
    wts_d = nc.dram_tensor("wts", [max(depth, 1), 128, TPP], F32, kind="ExternalInput").ap()
    small_d = nc.dram_tensor("small", [128, SM_TOT], F32, kind="ExternalInput").ap()
    consts_d = nc.dram_tensor("consts", [128, 512], F32, kind="ExternalInput").ap()
    ind_d = nc.dram_tensor("ind", [128, 2048], BF16, kind="ExternalInput").ap()
    maskh_d = nc.dram_tensor("maskh", [128, 1024], BF16, kind="ExternalInput").ap()
    outT_d = nc.dram_tensor("outT", [D, S], F32, kind="ExternalOutput").ap()

    es = ExitStack()
    with es:
        def sb(name, shape, dt):
            return es.enter_context(nc.sbuf_tensor("mk_" + name, shape, dt))
        h = sb("h_sb", [128, 8, S], F32)
        xn = sb("xn", [128, 8, 1024], BF16)
        RB = sb("RB", [128, 28672], BF16)
        G = sb("G", [128, 4, 30 + S], BF16)
        stg = [sb(f"stg{i}", [128, 1024], F32) for i in range(NS)]
        wb = [sb(f"wb{i}", [128, 1024], BF16) for i in range(NB)]
        SQ = sb("SQ", [128, 4096], BF16)
        tmp = [sb(f"tmp{i}", [128, 512], F32) for i in range(4)]
        ptl = [sb(f"pt{i}", [128, 512], BF16) for i in range(4)]
        qzb = [[sb(f"qz{a}{b}", [128, 512], BF16) for b in range(1)] for a in range(2)]
        ftok = sb("ftok", [128, 128], F32)
        sp_t = sb("sp_t", [128, 128], F32)
        tot = sb("tot", [128, 128], F32)
        pre = sb("pre", [128, 128], F32)
        cstok = sb("cstok", [128, 128], F32)
        refb = sb("refb", [128, 128], F32)
        cst = sb("cst", [128, 512], F32)
        tri_b = sb("tri_b", [128, 128], BF16)
        ones_b = sb("ones_b", [128, 128], BF16)
        ident_b = sb("ident_b", [128, 128], BF16)
        small = sb("small_sb", [128, SM_TOT], F32)
        dummy = sb("dummy", [128, 2], F32)
        ps = [es.enter_context(nc.psum_tensor(f"ps{i}", [128, 512], F32)) for i in range(8)]

        pbuf = [sb(f"pbuf{i}", [128, 1024], BF16) for i in range(2)]
        qT = RB[:, 0:8192].rearrange("p (c t) -> p c t", c=4)
        kT = RB[:, 8192:16384].rearrange("p (c t) -> p c t", c=4)
        vaug = RB[:, 16384:28672].rearrange("p (i c f) -> p i c f", i=16, c=4)
        hid = RB[:, 0:22528].rearrange("p (j t) -> p j t", j=NJ)
        mrg = RB[:, 8192:16384].rearrange("p (k t) -> p k t", k=8)
        diag = RB[:, 8192:8192 + 4 * CK * 128].rearrange("p (c k m) -> p c k m", c=4, k=CK)
        sq = SQ[:, :].rearrange("p (k t) -> p k t", k=8)
        bias = SQ[:, 0:1024].bitcast(F32).rearrange("p (j i h) -> p j i h", j=4, i=16)
        xnf = xn[:, :, :].rearrange("p k t -> p (k t)")
        INDv = xnf[:, 0:2048]
        MASKHv = xnf[:, 2048:3072].rearrange("p (h s) -> p h s", h=8)
        AUGL = xnf[:, 3072:4096].rearrange("p (h s) -> p h s", h=8)
        XNKEYS = [("xn", k, tl) for k in range(8) for tl in range(2)]
        ident_f = cst[:, 0:128]; tri_f = cst[:, 128:256]; e64_f = cst[:, 256:384]; ones_f = cst[:, 384:512]
        RALL = ["RQu", "RKu", "RVu"]

        P = Prog(nc, es)
        state = {"sn": 0, "bn": 0, "ps": 0, "tmp": 0, "pt": 0, "xb": 0, "sb": 0, "qz": [0, 0]}

        def mm(out, lhsT, rhs, start, stop, r, w):
            P.op("pe", lambda e: e.matmul(out, lhsT, rhs, start=start, stop=stop), r=r, w=w)

        def act(out, in_, func, r, w, **kw):
            P.op("act", lambda e: e.activation(out=out, in_=in_, func=func, **kw), r=r, w=w)

        def acopy(out, in_, r, w):
            P.op("act", lambda e: e.copy(out=out, in_=in_), r=r, w=w)

        def vcopy(out, in_, r, w):
            P.op("dve", lambda e: e.tensor_copy(out=out, in_=in_), r=r, w=w)

        def vtt(out, in0, in1, op, r, w):
            P.op("dve", lambda e: e.tensor_tensor(out=out, in0=in0, in1=in1, op=op), r=r, w=w)

        def vstt(out, in0, scalar, in1, op0, op1, r, w):
            P.op("dve", lambda e: e.scalar_tensor_tensor(out=out, in0=in0, scalar=scalar, in1=in1, op0=op0, op1=op1), r=r, w=w)

        def vts(out, in0, scalar1, op0, r, w):
            P.op("dve", lambda e: e.tensor_scalar(out=out, in0=in0, scalar1=scalar1, scalar2=None, op0=op0), r=r, w=w)

        def vmemset(ap, val, r, w):
            P.op("dve", lambda e: e.memset(ap, val), r=r, w=w)

        def vrecip(out, in_, r, w):
            P.op("dve", lambda e: e.reciprocal(out=out, in_=in_), r=r, w=w)

        def dma(out, in_, r, w):
            return P.op("sp", lambda e: e.dma_start(out=out, in_=in_), r=r, w=w, dma=True)

        def pcast(out, in_, r, w):
            P.op("pool", lambda e: e.tensor_copy(out=out, in_=in_), r=r, w=w)

        def fence(keys):
            vmemset(dummy[:, 0:1], 0.0, [], list(keys) + ["dummy"])

        def psb():
            i = state["ps"] % 8; state["ps"] += 1
            return i

        def tmpi():
            i = state["tmp"] % 4; state["tmp"] += 1
            return i

        def stage(src, ne):
            si = state["sn"] % NS; state["sn"] += 1
            dma(stg[si][:, 0:ne], src, [], [("stg", si)])
            return si

        def wtile(l, spec):
            _, k0, kc, c0, ncol = spec
            ne = kc * ncol
            si = stage(wts_d[l, :, offs[spec]:offs[spec] + ne], ne)
            bi = state["bn"] % NB; state["bn"] += 1
            pcast(wb[bi][:, 0:ne], stg[si][:, 0:ne], [("stg", si)], [("wb", bi)])
            return wb[bi][:, 0:ne].rearrange("p (k c) -> p k c", k=kc), ("wb", bi)

        def sm(l, off, n=1):
            return small[:, l * SM_L + off: l * SM_L + off + n]

        def tsl(t):
            return slice(t * TS, (t + 1) * TS)

        dma(small[:, :], small_d[:, :], [], ["small"])
        dma(cst[:, :], consts_d[:, :], [], ["cst"])
        for hh0 in range(2):
            for c in range(8):
                dma(h[:, c, hh0 * 1024:(hh0 + 1) * 1024], xT_d[c * 128:(c + 1) * 128, hh0 * 1024:(hh0 + 1) * 1024], [],
                    [("h", c, 2 * hh0), ("h", c, 2 * hh0 + 1)])
        P.op("dve", lambda e: e.tensor_scalar(out=tri_b[:, :], in0=tri_f, scalar1=-1.0, scalar2=30000.0, op0=ALU.add, op1=ALU.mult),
             r=["cst"], w=["tri_b"])
        vcopy(ident_b[:, :], ident_f, ["cst"], ["ident_b"])
        vcopy(ones_b[:, :], ones_f, ["cst"], ["ones_b"])
        vmemset(G[:, :, 0:30], 0.0, [], ["Gpad"])
        for a in range(2):
            for b in range(1):
                vmemset(qzb[a][b][(1 - a) * 64:(2 - a) * 64, :], 0.0, [], [("qz", a, b)])

        def rstd_tile(src_ap, rkeys, nk, inv_n, sq_kw=None):
            if sq_kw is None:
                act(sq[:, 0:nk, :], src_ap, AF.Square, list(rkeys) + ["SQu"], ["sq"])
            b = psb()
            for k in range(nk):
                mm(ps[b][:, :], ones_b[:, :], sq[:, k, :], k == 0, k == nk - 1, ["ones_b", "sq", "SQu"], [("ps", b)])
            ti = tmpi()
            act(tmp[ti][:, :], ps[b][:, :], AF.Ln, [("ps", b)], [("tmp", ti)], scale=inv_n, bias=EPS)
            act(tmp[ti][:, :], tmp[ti][:, :], AF.Exp, [("tmp", ti)], [("tmp", ti)], scale=-0.5)
            return ti

        def norm_half(l, hh, goff, dst=None, dkey="xn", extra_r=()):
            if dst is None:
                dst = xn
            for tl in range(2):
                t = 2 * hh + tl
                ti = rstd_tile(h[:, :, tsl(t)], [("h", k, t) for k in range(8)], 8, 1.0 / D)
                for k in range(8):
                    vstt(dst[:, k, tsl(tl)], h[:, k, tsl(t)], sm(l, goff + k), tmp[ti][:, :], ALU.mult, ALU.mult,
                         [("h", k, t), ("tmp", ti), "small"] + list(extra_r), [(dkey, k, tl)])

        def proj(wt, kw, nk, rhs_fn, rkeys_fn, b):
            for k in range(nk):
                mm(ps[b][:, :], wt[:, k, :], rhs_fn(k), k == 0, k == nk - 1, [kw] + rkeys_fn(k), [("ps", b)])

        def ffn(l, w_in_nm, w_out_nm, goff, pre0=False, hook=None):
            fence(RALL)
            for hh in range(2):
                if hh == 0 and not pre0:
                    norm_half(l, 0, goff)
                for j in range(NJ):
                    wa, ka = wtile(l, (w_in_nm, 0, 8, j * 128, 128))
                    wg, kg = wtile(l, (w_in_nm, 0, 8, DFF + j * 128, 128))
                    for tl in range(2):
                        ba = psb(); bb = psb()
                        proj(wa, ka, 8, lambda k: xn[:, k, tsl(tl)], lambda k: [("xn", k, tl)], ba)
                        proj(wg, kg, 8, lambda k: xn[:, k, tsl(tl)], lambda k: [("xn", k, tl)], bb)
                        ti = tmpi()
                        act(tmp[ti][:, :], ps[ba][:, :], AF.Silu, [("ps", ba)], [("tmp", ti)])
                        vtt(hid[:, j, tsl(tl)], ps[bb][:, :], tmp[ti][:, :], ALU.mult,
                            [("ps", bb), ("tmp", ti)] + RALL, [("hid", j, tl)])
                for m in range(8):
                    if hh == 0 and m == 3:
                        norm_half(l, 1, goff)
                    if hh == 1 and m == 3 and hook is not None:
                        hook()
                    banks = [psb(), psb()]
                    for (k0, kc) in ((0, 8), (8, 8), (16, 6)):
                        wt, kw = wtile(l, (w_out_nm, k0, kc, m * 128, 128))
                        for tl in range(2):
                            for kk in range(kc):
                                j = k0 + kk
                                mm(ps[banks[tl]][:, :], wt[:, kk, :], hid[:, j, tsl(tl)], j == 0, j == NJ - 1,
                                   [kw, ("hid", j, tl)] + RALL, [("ps", banks[tl])])
                    for tl in range(2):
                        t = 2 * hh + tl
                        vstt(h[:, m, tsl(t)], ps[banks[tl]][:, :], 0.5, h[:, m, tsl(t)], ALU.mult, ALU.add,
                             [("ps", banks[tl]), ("h", m, t)], [("h", m, t)])

        def mixer(l, pre0=False, hook=None):
            fence(RALL + ["SQu"])
            gmix = 8
            vmemset(vaug[:, :, :, 64:128], 1.0, ["RVu"], ["vones"])
            only = os.environ.get("MK_ONLY", "qkvfg")
            for hh in range(2):
                if not (hh == 0 and pre0):
                    norm_half(l, hh, gmix)
                for c in range(4 if "q" in only else 0):
                    wt, kw = wtile(l, ("w_in", 0, 8, Q0 + c * 128, 128))
                    for tl in range(2):
                        t = 2 * hh + tl
                        b = psb()
                        proj(wt, kw, 8, lambda k: xn[:, k, tsl(tl)], lambda k: [("xn", k, tl)], b)
                        acopy(qT[:, c, tsl(t)], ps[b][:, :], [("ps", b), "RQu"], [("q", c, t, 0), ("q", c, t, 1)])
                for c in range(4 if "k" in only else 0):
                    wt, kw = wtile(l, ("w_in", 0, 8, K0 + c * 128, 128))
                    for tl in range(2):
                        t = 2 * hh + tl
                        b = psb()
                        proj(wt, kw, 8, lambda k: xn[:, k, tsl(tl)], lambda k: [("xn", k, tl)], b)
                        vcopy(kT[:, c, tsl(t)], ps[b][:, :], [("ps", b), "RKu"], [("k", c, 4 * t + i) for i in range(4)])
                for c in range(4 if "v" in only else 0):
                    wt, kw = wtile(l, ("w_in", 0, 8, V0 + c * 128, 128))
                    for il in range(8):
                        i = 8 * hh + il
                        tl = il // 4
                        b = psb()
                        for k in range(8):
                            mm(ps[b][:, 0:128], xn[:, k, il * 128:(il + 1) * 128], wt[:, k, :], k == 0, k == 7,
                               [kw, ("xn", k, tl)], [("ps", b)])
                        if il % 2 == 0:
                            vcopy(vaug[:, i, c, 0:64], ps[b][:, 0:64], [("ps", b), "RVu"], [("v", i, c, 0)])
                            vcopy(vaug[:, i, c, 128:192], ps[b][:, 64:128], [("ps", b), "RVu"], [("v", i, c, 1)])
                        else:
                            acopy(vaug[:, i, c, 0:64], ps[b][:, 0:64], [("ps", b), "RVu"], [("v", i, c, 0)])
                            acopy(vaug[:, i, c, 128:192], ps[b][:, 64:128], [("ps", b), "RVu"], [("v", i, c, 1)])
                wt, kw = wtile(l, ("w_in", 0, 8, F0, 8))
                b = psb()
                for il in range(8 if "f" in only else 0):
                    tl = il // 4
                    for k in range(8):
                        mm(ps[b][:, il * 8:(il + 1) * 8], xn[:, k, il * 128:(il + 1) * 128], wt[:, k, :], k == 0, k == 7,
                           [kw, ("xn", k, tl)], [("ps", b)])
                if "f" in only:
                    vtt(ftok[:, hh * 64:(hh + 1) * 64], ps[b][:, 0:64], sm(l, 164, 64), ALU.add,
                        [("ps", b), "small"], [("ftok", hh)])
                for c in range(4 if "g" in only else 0):
                    wa, ka = wtile(l, ("w_in", 0, 8, C0 + c * 128, 128))
                    wg, kg = wtile(l, ("w_in", 0, 8, C0 + CC + c * 128, 128))
                    for tl in range(2):
                        t = 2 * hh + tl
                        ba = psb(); bb = psb()
                        proj(wa, ka, 8, lambda k: xn[:, k, tsl(tl)], lambda k: [("xn", k, tl)], ba)
                        proj(wg, kg, 8, lambda k: xn[:, k, tsl(tl)], lambda k: [("xn", k, tl)], bb)
                        ti = tmpi()
                        act(tmp[ti][:, :], ps[bb][:, :], AF.Sigmoid, [("ps", bb)], [("tmp", ti)])
                        vtt(G[:, c, 30 + t * TS:30 + (t + 1) * TS], ps[ba][:, :], tmp[ti][:, :], ALU.mult,
                            [("ps", ba), ("tmp", ti)], [("G", c, t)])
            mixlvl = int(os.environ.get("MK_MIX", "9"))
            if mixlvl < 2:
                return
            act(sp_t[:, :], ftok[:, :], AF.Exp, [("ftok", 0), ("ftok", 1)], ["sp_t"], scale=-1.0)
            act(sp_t[:, :], sp_t[:, :], AF.Ln, ["sp_t"], ["sp_t"], scale=1.0, bias=1.0)
            b1 = psb(); b2 = psb()
            mm(ps[b1][:, 0:128], tri_f, sp_t[:, :], True, True, ["cst", "sp_t"], [("ps", b1)])
            mm(ps[b2][:, 0:128], ones_f, sp_t[:, :], True, True, ["cst", "sp_t"], [("ps", b2)])
            vcopy(tot[:, :], ps[b2][:, 0:128], [("ps", b2)], ["tot"])
            tot3 = tot[:, :].rearrange("p (i h) -> p i h", i=16)
            pre3 = pre[:, :].rearrange("p (i h) -> p i h", i=16)
            for hd in range(NH):
                P.op("dve", lambda e, hd=hd: e.tensor_tensor_scan(out=pre3[:, :, hd], data0=ones_f[:, 0:16], data1=tot3[:, :, hd],
                                                                  initial=0.0, op0=ALU.mult, op1=ALU.add),
                     r=["tot", "cst", "pre"], w=[("pre", hd)])
            vtt(pre[:, :], pre[:, :], tot[:, :], ALU.subtract, [("pre", hd) for hd in range(NH)] + ["tot"],
                ["pre"] + [("pre", hd) for hd in range(NH)])
            vtt(cstok[:, :], ps[b1][:, 0:128], pre[:, :], ALU.add, [("ps", b1), "pre"], ["cstok"])
            b3 = psb()
            mm(ps[b3][:, 0:128], e64_f, cstok[:, :], True, True, ["cst", "cstok"], [("ps", b3)])
            vcopy(refb[:, :], ps[b3][:, 0:128], [("ps", b3)], ["refb"])
            for bq in range(16):
                jr = 4 * (bq // 4) + 2
                vtt(tot[:, bq * 8:(bq + 1) * 8], refb[:, bq * 8:(bq + 1) * 8], refb[:, jr * 8:(jr + 1) * 8], ALU.subtract,
                    ["refb", "tot"], ["tot"])
            fence(XNKEYS + ["xnalias"])
            dma(INDv, ind_d[:, :], ["xnalias"], ["ind"])
            dma(MASKHv.rearrange("p h s -> p (h s)"), maskh_d[:, :], ["xnalias"], ["maskh"])
            bt = psb()
            P.op("pe", lambda e: e.transpose(ps[bt][:, 0:128], tot[:, :], ident_f), r=["tot", "cst"], w=[("ps", bt)])
            for hd in range(NH):
                P.op("dve", lambda e, hd=hd: e.tensor_scalar(out=AUGL[:, hd, :], in0=MASKHv[:, hd, :], scalar1=ps[bt][:, 0:1],
                                                             scalar2=-8.0, op0=ALU.mult, op1=ALU.mult),
                     r=[("ps", bt), "maskh", "xnalias"], w=[("augl", hd)])
            fence(["SQu"])
            cs3 = cstok[:, :].rearrange("p (i h) -> p i h", i=16)
            for j in range(4):
                for hd in range(NH):
                    col = (4 * j + 2) * 8 + hd
                    vts(bias[:, j, 0:4 * j + 4, hd], cs3[:, 0:4 * j + 4, hd], refb[:, col:col + 1], ALU.subtract,
                        ["cstok", "refb", "SQu"], [("bias", j)])
            if mixlvl < 3:
                return
            units = [(c, j, hx) for c in range(4) for j in range(4) for hx in range(2)]

            def qz_copy(u):
                c_, j_, hx_ = u
                pr = slice(hx_ * 64, (hx_ + 1) * 64)
                pcast(qzb[hx_][0][pr, :], qT[pr, c_, tsl(j_)], [("q", c_, j_, hx_), "RQu"], [("qz", hx_, 0)])
            qz_copy(units[0])
            for ui, (c, j, hx) in enumerate(units):
                if ui + 1 < len(units):
                    qz_copy(units[ui + 1])
                attn_unit(c, j, hx)
            fence(XNKEYS + ["xnalias", "SQu"])
            if mixlvl < 4:
                return
            norm_half(l, 0, gmix)
            fence(["RKu", "RVu"])
            for c in range(4):
                for k in range(CK):
                    if k % 2 == 0:
                        vts(diag[:, c, k, :], ident_f, sm(l, 40 + c * CK + k), ALU.mult,
                            ["cst", "small", "RKu", "RVu"], [("diag", c)])
                    else:
                        act(diag[:, c, k, :], ident_f, AF.Identity, ["cst", "small", "RKu", "RVu"], [("diag", c)],
                            scale=sm(l, 40 + c * CK + k))
            for t in (3, 2, 1, 0):
                cb = [psb() for _ in range(4)]
                for c in range(4):
                    rk = [("G", c, t)] + ([("G", c, t - 1)] if t > 0 else ["Gpad"])
                    for k in range(CK):
                        mm(ps[cb[c]][:, :], diag[:, c, k, :], G[:, c, t * TS + k:t * TS + k + TS], k == 0, k == CK - 1,
                           [("diag", c), "RKu", "RVu"] + rk, [("ps", cb[c])])
                for c in range(4):
                    act(sq[:, c, :], ps[cb[c]][:, :], AF.Square, [("ps", cb[c]), "small", "SQu"], ["sq"],
                        bias=sm(l, 32 + c), scale=1.0)
                ti = rstd_tile(None, [], 4, 1.0 / CC, sq_kw=True)
                for c in range(4):
                    t2 = tmpi()
                    vstt(tmp[t2][:, :], ps[cb[c]][:, :], sm(l, 32 + c), tmp[ti][:, :], ALU.add, ALU.mult,
                         [("ps", cb[c]), ("tmp", ti), "small"], [("tmp", t2)])
                    act(G[:, c, 30 + t * TS:30 + (t + 1) * TS], tmp[t2][:, :], AF.Silu, [("tmp", t2), "small"], [("G", c, t)],
                        scale=sm(l, 36 + c))
            if mixlvl < 5:
                return
            fence(["RKu", "RVu"])
            for hh in range(2):
                for m in range(8):
                    tt_ = []
                    for (wnm, gc0, src_is_attn) in (("w_attn_out", GA0, True), ("w_conv_out", GC0, False)):
                        wy, ky = wtile(l, (wnm, 0, 4, m * 128, 128))
                        wgt, kgt = wtile(l, ("w_in", 0, 8, gc0 + m * 128, 128))
                        row = []
                        for tl in range(2):
                            t = 2 * hh + tl
                            by = psb(); bg = psb()
                            if src_is_attn:
                                proj(wy, ky, 4, lambda k: qT[:, k, tsl(t)], lambda k: [("q", k, t, 0), ("q", k, t, 1), "RQu"], by)
                            else:
                                proj(wy, ky, 4, lambda k: G[:, k, 30 + t * TS:30 + (t + 1) * TS], lambda k: [("G", k, t)], by)
                            proj(wgt, kgt, 8, lambda k: xn[:, k, tsl(tl)], lambda k: [("xn", k, tl)], bg)
                            t1 = tmpi()
                            act(tmp[t1][:, :], ps[bg][:, :], AF.Sigmoid, [("ps", bg)], [("tmp", t1)])
                            if src_is_attn:
                                vtt(mrg[:, m, tsl(tl)], ps[by][:, :], tmp[t1][:, :], ALU.mult,
                                    [("ps", by), ("tmp", t1), "RKu"], [("mrgA", m, tl), ("mrg", m, tl)])
                            else:
                                vtt(tmp[t1][:, :], ps[by][:, :], tmp[t1][:, :], ALU.mult,
                                    [("ps", by), ("tmp", t1)], [("tmp", t1)])
                                vtt(mrg[:, m, tsl(tl)], mrg[:, m, tsl(tl)], tmp[t1][:, :], ALU.add,
                                    [("mrgA", m, tl), ("tmp", t1), "RKu"], [("mrg", m, tl), ("mrgA", m, tl)])
                for m in range(8):
                    if hh == 0 and m == 2:
                        norm_half(l, 1, gmix)
                    if hh == 1 and m == 2 and hook is not None:
                        hook()
                    wo, ko = wtile(l, ("w_out", 0, 8, m * 128, 128))
                    for tl in range(2):
                        t = 2 * hh + tl
                        b = psb()
                        proj(wo, ko, 8, lambda k: mrg[:, k, tsl(tl)], lambda k: [("mrg", k, tl), "RKu"], b)
                        vtt(h[:, m, tsl(t)], ps[b][:, :], h[:, m, tsl(t)], ALU.add, [("ps", b), ("h", m, t)], [("h", m, t)])

        def attn_unit(c, j, hx):
            hd = 2 * c + hx
            prow = slice(hx * 64, (hx + 1) * 64)
            drow = slice((1 - hx) * 64, (2 - hx) * 64)
            lo = 0 if hx == 0 else 64
            xb = 6 + (state["xb"] % 2); state["xb"] += 1
            zi = 0
            qz = qzb[hx][zi]
            nki = 4 * j + 4
            pend = []

            def qk(i):
                q0 = max(0, i - 4 * j)
                sbk = state["sb"] % 6; state["sb"] += 1
                mm(ps[sbk][:, q0 * 128:512], kT[:, c, i * 128:(i + 1) * 128],
                   qz[:, q0 * 128:512], True, False,
                   [("k", c, i), ("qz", hx, zi), "RKu"], [("ps", sbk)])
                diag_blk = i >= 4 * j
                mm(ps[sbk][:, q0 * 128:512], AUGL[:, hd, :], INDv[:, j * TS + q0 * 128:(j + 1) * TS], False, not diag_blk,
                   [("augl", hd), "ind", "xnalias"], [("ps", sbk)])
                if diag_blk:
                    mm(ps[sbk][:, q0 * 128:(q0 + 1) * 128], ident_b[:, :], tri_b[:, :], False, True,
                       ["ident_b", "tri_b"], [("ps", sbk)])
                pi = state["pt"] % 4; state["pt"] += 1
                act(ptl[pi][:, q0 * 128:512], ps[sbk][:, q0 * 128:512], AF.Exp,
                    [("ps", sbk), ("bias", j), "SQu"], [("pt", pi)],
                    scale=0.125, bias=bias[:, j, i, hd:hd + 1])
                return (i, q0, pi)

            def pv(item):
                i, q0, pi = item
                mm(ps[xb][:, q0 * 128:512], vaug[:, i, c, lo:lo + 128], ptl[pi][:, q0 * 128:512],
                   i == 0, i == nki - 1, [("v", i, c, hx), "vones", ("pt", pi), "RVu"], [("ps", xb)])

            for i in range(nki):
                pend.append(qk(i))
                if len(pend) > 3:
                    pv(pend.pop(0))
            while pend:
                pv(pend.pop(0))
            ti = tmpi()
            vrecip(tmp[ti][prow, :], ps[xb][drow, :], [("ps", xb)], [("tmp", ti)])
            vtt(qT[prow, c, tsl(j)], ps[xb][prow, :], tmp[ti][prow, :], ALU.mult,
                [("ps", xb), ("tmp", ti), "RQu"], [("q", c, j, hx)])

        xn2 = RB[:, 0:8192].rearrange("p (k t) -> p k t", k=8)

        def ple(l, pre0=False, hook=None):
            fence(["RQu"])
            if not pre0:
                norm_half(l, 0, 24)
            norm_half(l, 1, 24, dst=xn2, dkey="xn2", extra_r=["RQu"])
            for hh in range(2):
                xsrc = xn if hh == 0 else xn2
                xkey = "xn" if hh == 0 else "xn2"
                xextra = [] if hh == 0 else ["RQu"]
                for k in range(2):
                    si = stage(pT_d[l, hh, :, k, :], 1024)
                    pcast(pbuf[k][:, :], stg[si][:, 0:1024], [("stg", si)], [("pbuf", k)])
                for m in range(8):
                    if hh == 1 and m == 2 and hook is not None:
                        hook()
                    wg, kg = wtile(l, ("w_ple_gate", 0, 8, m * 128, 128))
                    wp, kp = wtile(l, ("w_ple_proj", 0, 2, m * 128, 128))
                    for tl in range(2):
                        t = 2 * hh + tl
                        bg = psb(); bp = psb()
                        proj(wg, kg, 8, lambda k: xsrc[:, k, tsl(tl)], lambda k: [(xkey, k, tl)] + xextra, bg)
                        proj(wp, kp, 2, lambda k: pbuf[k][:, tsl(tl)], lambda k: [("pbuf", k)], bp)
                        t1 = tmpi()
                        act(tmp[t1][:, :], ps[bg][:, :], AF.Sigmoid, [("ps", bg)], [("tmp", t1)])
                        vtt(tmp[t1][:, :], ps[bp][:, :], tmp[t1][:, :], ALU.mult, [("ps", bp), ("tmp", t1)], [("tmp", t1)])
                        vtt(h[:, m, tsl(t)], tmp[t1][:, :], h[:, m, tsl(t)], ALU.add, [("tmp", t1), ("h", m, t)], [("h", m, t)])

        for l in range(depth):
            P.phase = l
            parts = os.environ.get("MK_PARTS", "fmgp")
            full = parts == "fmgp"
            if "f" in parts:
                ffn(l, "w_ff1_in", "w_ff1_out", 0, full and l > 0, (lambda l=l: norm_half(l, 0, 8)) if full else None)
            if "m" in parts:
                mixer(l, full, (lambda l=l: norm_half(l, 0, 16)) if full else None)
            if "g" in parts:
                ffn(l, "w_ff2_in", "w_ff2_out", 16, full, (lambda l=l: norm_half(l, 0, 24)) if full else None)
            if "p" in parts:
                ple(l, full, (lambda l=l: norm_half(l + 1, 0, 0)) if (full and l + 1 < depth) else None)

        P.phase = depth
        gf = SM_L * L
        for t in range(4):
            ti = rstd_tile(h[:, :, tsl(t)], [("h", k, t) for k in range(8)], 8, 1.0 / D)
            for k in range(8):
                vstt(h[:, k, tsl(t)], h[:, k, tsl(t)], small[:, gf + k:gf + k + 1], tmp[ti][:, :], ALU.mult, ALU.mult,
                     [("h", k, t), ("tmp", ti), "small"], [("h", k, t)])
            dma(outT_d.rearrange("(c p) s -> p c s", p=128)[:, :, tsl(t)], h[:, :, tsl(t)],
                [("h", k, t) for k in range(8)], [("out", t)])
        P.op("sp", None, r=[("out", t) for t in range(4)])
        P.emit()
    return nc


def _pack(inputs, depth):
    specs, offs, TPP = layer_tile_specs()
    wts = np.zeros((max(depth, 1), 128, TPP), np.float32)
    for l in range(depth):
        for sp in specs:
            nm, k0, kc, c0, ncol = sp
            W = inputs[nm][l]
            blk = W[k0 * 128:(k0 + kc) * 128, c0:c0 + ncol].reshape(kc, 128, ncol).transpose(1, 0, 2).reshape(128, kc * ncol)
            wts[l, :, offs[sp]:offs[sp] + kc * ncol] = blk
    small = np.zeros((128, SM_TOT), np.float32)

    def fm(v, nchunk):
        return np.asarray(v, np.float32).reshape(nchunk, 128).T
    for l in range(L):
        o = l * SM_L
        small[:, o + 0:o + 8] = fm(inputs["g_ff1"][l], 8)
        small[:, o + 8:o + 16] = fm(inputs["g_mix"][l], 8)
        small[:, o + 16:o + 24] = fm(inputs["g_ff2"][l], 8)
        small[:, o + 24:o + 32] = fm(inputs["g_ple"][l], 8)
        small[:, o + 32:o + 36] = fm(inputs["conv_b"][l], 4)
        small[:, o + 36:o + 40] = fm(inputs["g_conv"][l], 4)
        cw = np.asarray(inputs["conv_w"][l], np.float32)
        small[:, o + 40:o + 164] = cw.reshape(CK, 4, 128).transpose(2, 1, 0).reshape(128, 4 * CK)
        small[:, o + 164:o + 228] = np.tile(np.asarray(inputs["b_f"][l], np.float32)[None, :], (128, 8))
    small[:, SM_L * L:] = fm(inputs["g_final"], 8)
    consts = np.zeros((128, 512), np.float32)
    consts[:, 0:128] = np.eye(128, dtype=np.float32)
    consts[:, 128:256] = np.triu(np.ones((128, 128), np.float32))
    consts[64, 256:384] = 1.0
    consts[:, 384:512] = 1.0
    return wts, small, consts


def kernel(**inputs):
    depth = int(os.environ.get("MK_DEPTH", L))
    inputs = {k: np.asarray(v) for k, v in inputs.items()}
    wts, small, consts = _pack(inputs, depth)
    x = inputs["x"].astype(np.float32, copy=False)
    p = inputs["p"].astype(np.float32, copy=False)
    nc = build_program(depth)
    import ml_dtypes
    pidx = np.arange(128)
    ind = (pidx[:, None] // 8 == (np.arange(2048)[None, :] // 128)).astype(ml_dtypes.bfloat16)
    maskh = np.repeat((pidx[:, None] % 8 == np.arange(8)[None, :]).astype(ml_dtypes.bfloat16), 128, axis=1)
    in_maps = []
    for b in range(8):
        xT = np.ascontiguousarray(x[b].T)
        pT = np.ascontiguousarray(p[:, b].reshape(L, 2, 1024, 2, 128).transpose(0, 1, 4, 3, 2))
        in_maps.append({"xT": xT, "pT": pT, "wts": wts, "small": small, "consts": consts, "ind": ind, "maskh": maskh})
    res = run_bass_kernel_spmd(nc, in_maps, core_ids=list(range(8)))
    out = np.stack([np.ascontiguousarray(r["outT"].T) for r in res.results], axis=0)
    return out.astype(np.float32, copy=False)
```

```python
import os
import numpy as np
from contextlib import ExitStack
import concourse.bass as bass
import concourse.mybir as mybir
from concourse.bass_utils import run_bass_kernel_spmd

F32 = mybir.dt.float32
BF16 = mybir.dt.bfloat16
AF = mybir.ActivationFunctionType
ALU = mybir.AluOpType

D = 1024; S = 2048; L = 4; NH = 8; DH = 64; DFF = 2816; NJ = 22; DPLE = 256; CK = 31; CC = 512
Q0 = 0; K0 = 512; V0 = 1024; F0 = 1536; C0 = 1544; GA0 = 2568; GC0 = 3592
EPS = 1e-6
TS = 512
NDMA = 8
NS = 3
NB = 4
SM_L = 32 + 8 + 124 + 64
SM_TOT = SM_L * L + 8


class Op:
    __slots__ = ("eng", "fn", "deps", "sig", "needed", "dma", "dma_slot", "phase")


class Prog:
    def __init__(self, nc, es):
        self.nc = nc; self.es = es; self.ops = []
        self.last_w = {}; self.readers = {}
        self.phase = 0
        self.dma_last = [None] * NDMA
        self.dma_rr = 0

    def op(self, eng, fn, r=(), w=(), dma=False):
        i = len(self.ops)
        deps = set()
        rk0 = ("dma", i) if dma else eng
        for k in r:
            if k in self.last_w:
                deps.add(self.last_w[k])
            if isinstance(k, tuple) and k[0] == "ps":
                for e2, idx in self.readers.get(k, {}).items():
                    if e2 != rk0:
                        deps.add(idx)
        for k in w:
            if k in self.last_w:
                deps.add(self.last_w[k])
            deps.update(self.readers.get(k, {}).values())
        rk = ("dma", i) if dma else eng
        for k in r:
            self.readers.setdefault(k, {})[rk] = i
        for k in w:
            self.last_w[k] = i
            self.readers[k] = {}
        o = Op(); o.eng = eng; o.fn = fn; o.dma = dma; o.phase = self.phase; o.dma_slot = -1
        if dma:
            slot = self.dma_rr % NDMA; self.dma_rr += 1
            if self.dma_last[slot] is not None:
                deps.add(self.dma_last[slot])
            self.dma_last[slot] = i
            o.dma_slot = slot
        o.deps = [d for d in deps if not (self.ops[d].eng == "pe" and eng == "pe")]
        self.ops.append(o)
        return i

    def emit(self):
        nc = self.nc; ops = self.ops
        for o in ops:
            o.needed = False
        for o in ops:
            for d in o.deps:
                ops[d].needed = True
        sems = []
        semmap = {}
        cnt = {}
        dsem = []
        for i in range(NDMA):
            sems.append(self.es.enter_context(nc.semaphore(f"dma{i}")))
            dsem.append(len(sems) - 1)
        dcnt = [0] * NDMA
        for o in ops:
            if o.dma:
                dcnt[o.dma_slot] += 16
                o.sig = (dsem[o.dma_slot], dcnt[o.dma_slot])
            elif o.needed:
                key = (o.eng, o.phase)
                if key not in semmap:
                    sems.append(self.es.enter_context(nc.semaphore(f"s_{o.eng}_{o.phase}")))
                    semmap[key] = len(sems) - 1
                cnt[key] = cnt.get(key, 0) + 1
                o.sig = (semmap[key], cnt[key])
            else:
                o.sig = None
        with nc.Block() as block:
            def make(eng):
                def body(e):
                    known = {}
                    for o in ops:
                        if o.eng != eng:
                            continue
                        need = {}
                        for d in o.deps:
                            si, val = ops[d].sig
                            if known.get(si, 0) >= val:
                                continue
                            if need.get(si, 0) < val:
                                need[si] = val
                        for si, val in need.items():
                            e.wait_ge(sems[si], val)
                            known[si] = val
                        if o.fn is not None:
                            ins = o.fn(e)
                            if o.sig is not None:
                                ins.then_inc(sems[o.sig[0]], 16 if o.dma else 1)
                return body
            block.sync(make("sp"))
            block.tensor(make("pe"))
            block.scalar(make("act"))
            block.vector(make("dve"))
            block.gpsimd(make("pool"))


def layer_tile_specs():
    specs = []
    for nm in ("w_ff1_in", "w_ff2_in"):
        for j in range(NJ):
            specs.append((nm, 0, 8, j * 128, 128))
            specs.append((nm, 0, 8, DFF + j * 128, 128))
    for nm in ("w_ff1_out", "w_ff2_out"):
        for m in range(8):
            for (k0, kc) in ((0, 8), (8, 8), (16, 6)):
                specs.append((nm, k0, kc, m * 128, 128))
    for c0 in range(0, F0, 128):
        specs.append(("w_in", 0, 8, c0, 128))
    specs.append(("w_in", 0, 8, F0, 8))
    for c0 in range(C0, D * 2 + GA0, 128):
        specs.append(("w_in", 0, 8, c0, 128))
    for m in range(8):
        specs.append(("w_attn_out", 0, 4, m * 128, 128))
        specs.append(("w_conv_out", 0, 4, m * 128, 128))
        specs.append(("w_out", 0, 8, m * 128, 128))
        specs.append(("w_ple_gate", 0, 8, m * 128, 128))
        specs.append(("w_ple_proj", 0, 2, m * 128, 128))
    offs = {}
    off = 0
    for sp in specs:
        offs[sp] = off
        off += sp[2] * sp[4]
    return specs, offs, off


def build_program(depth):
    specs, offs, TPP = layer_tile_specs()
    nc = bass.Bass("TRN2", target_bir_lowering=False)
    xT_d = nc.dram_tensor("xT", [D, S], F32, kind="ExternalInput").ap()
    pT_d = nc.dram_tensor("pT", [L, 2, 128, 2, 1024], F32, kind="ExternalInput").ap()
    wts_d = nc.dram_tensor("wts", [max(depth, 1), 128, TPP], F32, kind="ExternalInput").ap()
    small_d = nc.dram_tensor("small", [128, SM_TOT], F32, kind="ExternalInput").ap()
    consts_d = nc.dram_tensor("consts", [128, 512], F32, kind="ExternalInput").ap()
    ind_d = nc.dram_tensor("ind", [128, 2048], BF16, kind="ExternalInput").ap()
    maskh_d = nc.dram_tensor("maskh", [128, 1024], BF16, kind="ExternalInput").ap()
    outT_d = nc.dram_tensor("outT", [D, S], F32, kind="ExternalOutput").ap()

    es = ExitStack()
    with es:
        def sb(name, shape, dt):
            return es.enter_context(nc.sbuf_tensor("mk_" + name, shape, dt))
        h = sb("h_sb", [128, 8, S], F32)
        xn = sb("xn", [128, 8, 1024], BF16)
        RB = sb("RB", [128, 28672], BF16)
        G = sb("G", [128, 4, 30 + S], BF16)
        stg = [sb(f"stg{i}", [128, 1024], F32) for i in range(NS)]
        wb = [sb(f"wb{i}", [128, 1024], BF16) for i in range(NB)]
        SQ = sb("SQ", [128, 4096], BF16)
        tmp = [sb(f"tmp{i}", [128, 512], F32) for i in range(4)]
        ptl = [sb(f"pt{i}", [128, 512], BF16) for i in range(4)]
        qzb = [[sb(f"qz{a}{b}", [128, 512], BF16) for b in range(1)] for a in range(2)]
        ftok = sb("ftok", [128, 128], F32)
        sp_t = sb("sp_t", [128, 128], F32)
        tot = sb("tot", [128, 128], F32)
        pre = sb("pre", [128, 128], F32)
        cstok = sb("cstok", [128, 128], F32)
        refb = sb("refb", [128, 128], F32)
        cst = sb("cst", [128, 512], F32)
        tri_b = sb("tri_b", [128, 128], BF16)
        ones_b = sb("ones_b", [128, 128], BF16)
        ident_b = sb("ident_b", [128, 128], BF16)
        small = sb("small_sb", [128, SM_TOT], F32)
        dummy = sb("dummy", [128, 2], F32)
        ps = [es.enter_context(nc.psum_tensor(f"ps{i}", [128, 512], F32)) for i in range(8)]

        pbuf = [sb(f"pbuf{i}", [128, 1024], BF16) for i in range(2)]
        qT = RB[:, 0:8192].rearrange("p (c t) -> p c t", c=4)
        kT = RB[:, 8192:16384].rearrange("p (c t) -> p c t", c=4)
        vaug = RB[:, 16384:28672].rearrange("p (i c f) -> p i c f", i=16, c=4)
        hid = RB[:, 0:22528].rearrange("p (j t) -> p j t", j=NJ)
        mrg = RB[:, 8192:16384].rearrange("p (k t) -> p k t", k=8)
        diag = RB[:, 8192:8192 + 4 * CK * 128].rearrange("p (c k m) -> p c k m", c=4, k=CK)
        sq = SQ[:, :].rearrange("p (k t) -> p k t", k=8)
        bias = SQ[:, 0:1024].bitcast(F32).rearrange("p (j i h) -> p j i h", j=4, i=16)
        xnf = xn[:, :, :].rearrange("p k t -> p (k t)")
        INDv = xnf[:, 0:2048]
        MASKHv = xnf[:, 2048:3072].rearrange("p (h s) -> p h s", h=8)
        AUGL = xnf[:, 3072:4096].rearrange("p (h s) -> p h s", h=8)
        XNKEYS = [("xn", k, tl) for k in range(8) for tl in range(2)]
        ident_f = cst[:, 0:128]; tri_f = cst[:, 128:256]; e64_f = cst[:, 256:384]; ones_f = cst[:, 384:512]
        RALL = ["RQu", "RKu", "RVu"]

        P = Prog(nc, es)
        state = {"sn": 0, "bn": 0, "ps": 0, "tmp": 0, "pt": 0, "xb": 0, "sb": 0, "qz": [0, 0]}

        def mm(out, lhsT, rhs, start, stop, r, w):
            P.op("pe", lambda e: e.matmul(out, lhsT, rhs, start=start, stop=stop), r=r, w=w)

        def act(out, in_, func, r, w, **kw):
            P.op("act", lambda e: e.activation(out=out, in_=in_, func=func, **kw), r=r, w=w)

        def acopy(out, in_, r, w):
            P.op("act", lambda e: e.copy(out=out, in_=in_), r=r, w=w)

        def vcopy(out, in_, r, w):
            P.op("dve", lambda e: e.tensor_copy(out=out, in_=in_), r=r, w=w)

        def vtt(out, in0, in1, op, r, w):
            P.op("dve", lambda e: e.tensor_tensor(out=out, in0=in0, in1=in1, op=op), r=r, w=w)

        def vstt(out, in0, scalar, in1, op0, op1, r, w):
            P.op("dve", lambda e: e.scalar_tensor_tensor(out=out, in0=in0, scalar=scalar, in1=in1, op0=op0, op1=op1), r=r, w=w)

        def vts(out, in0, scalar1, op0, r, w):
            P.op("dve", lambda e: e.tensor_scalar(out=out, in0=in0, scalar1=scalar1, scalar2=None, op0=op0), r=r, w=w)

        def vmemset(ap, val, r, w):
            P.op("dve", lambda e: e.memset(ap, val), r=r, w=w)

        def vrecip(out, in_, r, w):
            P.op("dve", lambda e: e.reciprocal(out=out, in_=in_), r=r, w=w)

        def dma(out, in_, r, w):
            return P.op("sp", lambda e: e.dma_start(out=out, in_=in_), r=r, w=w, dma=True)

        def pcast(out, in_, r, w):
            P.op("pool", lambda e: e.tensor_copy(out=out, in_=in_), r=r, w=w)

        def fence(keys):
            vmemset(dummy[:, 0:1], 0.0, [], list(keys) + ["dummy"])

        def psb():
            i = state["ps"] % 8; state["ps"] += 1
            return i

        def tmpi():
            i = state["tmp"] % 4; state["tmp"] += 1
            return i

        def stage(src, ne):
            si = state["sn"] % NS; state["sn"] += 1
            dma(stg[si][:, 0:ne], src, [], [("stg", si)])
            return si

        def wtile(l, spec, on_act=False):
            _, k0, kc, c0, ncol = spec
            ne = kc * ncol
            si = stage(wts_d[l, :, offs[spec]:offs[spec] + ne], ne)
            bi = state["bn"] % NB; state["bn"] += 1
            if on_act:
                acopy(wb[bi][:, 0:ne], stg[si][:, 0:ne], [("stg", si)], [("wb", bi)])
            else:
                pcast(wb[bi][:, 0:ne], stg[si][:, 0:ne], [("stg", si)], [("wb", bi)])
            return wb[bi][:, 0:ne].rearrange("p (k c) -> p k c", k=kc), ("wb", bi)

        def sm(l, off, n=1):
            return small[:, l * SM_L + off: l * SM_L + off + n]

        def tsl(t):
            return slice(t * TS, (t + 1) * TS)

        dma(small[:, :], small_d[:, :], [], ["small"])
        dma(cst[:, :], consts_d[:, :], [], ["cst"])
        for hh0 in range(2):
            for c in range(8):
                dma(h[:, c, hh0 * 1024:(hh0 + 1) * 1024], xT_d[c * 128:(c + 1) * 128, hh0 * 1024:(hh0 + 1) * 1024], [],
                    [("h", c, 2 * hh0), ("h", c, 2 * hh0 + 1)])
        P.op("dve", lambda e: e.tensor_scalar(out=tri_b[:, :], in0=tri_f, scalar1=-1.0, scalar2=30000.0, op0=ALU.add, op1=ALU.mult),
             r=["cst"], w=["tri_b"])
        vcopy(ident_b[:, :], ident_f, ["cst"], ["ident_b"])
        vcopy(ones_b[:, :], ones_f, ["cst"], ["ones_b"])
        vmemset(G[:, :, 0:30], 0.0, [], ["Gpad"])
        for a in range(2):
            for b in range(1):
                vmemset(qzb[a][b][(1 - a) * 64:(2 - a) * 64, :], 0.0, [], [("qz", a, b)])

        def rstd_tile(src_ap, rkeys, nk, inv_n, sq_kw=None):
            if sq_kw is None:
                act(sq[:, 0:nk, :], src_ap, AF.Square, list(rkeys) + ["SQu"], ["sq"])
            b = psb()
            for k in range(nk):
                mm(ps[b][:, :], ones_b[:, :], sq[:, k, :], k == 0, k == nk - 1, ["ones_b", "sq", "SQu"], [("ps", b)])
            ti = tmpi()
            act(tmp[ti][:, :], ps[b][:, :], AF.Ln, [("ps", b)], [("tmp", ti)], scale=inv_n, bias=EPS)
            act(tmp[ti][:, :], tmp[ti][:, :], AF.Exp, [("tmp", ti)], [("tmp", ti)], scale=-0.5)
            return ti

        def norm_half(l, hh, goff):
            for tl in range(2):
                t = 2 * hh + tl
                ti = rstd_tile(h[:, :, tsl(t)], [("h", k, t) for k in range(8)], 8, 1.0 / D)
                for k in range(8):
                    vstt(xn[:, k, tsl(tl)], h[:, k, tsl(t)], sm(l, goff + k), tmp[ti][:, :], ALU.mult, ALU.mult,
                         [("h", k, t), ("tmp", ti), "small"], [("xn", k, tl)])

        def proj(wt, kw, nk, rhs_fn, rkeys_fn, b):
            for k in range(nk):
                mm(ps[b][:, :], wt[:, k, :], rhs_fn(k), k == 0, k == nk - 1, [kw] + rkeys_fn(k), [("ps", b)])

        def ffn(l, w_in_nm, w_out_nm, goff, pre0=False, hook=None):
            fence(RALL)
            for hh in range(2):
                if hh == 0 and not pre0:
                    norm_half(l, 0, goff)
                nxt = (wtile(l, (w_in_nm, 0, 8, 0, 128)), wtile(l, (w_in_nm, 0, 8, DFF, 128), on_act=True))
                for j in range(NJ):
                    (wa, ka), (wg, kg) = nxt
                    if j + 1 < NJ:
                        nxt = (wtile(l, (w_in_nm, 0, 8, (j + 1) * 128, 128)),
                               wtile(l, (w_in_nm, 0, 8, DFF + (j + 1) * 128, 128), on_act=True))
                    for tl in range(2):
                        ba = psb(); bb = psb()
                        proj(wa, ka, 8, lambda k: xn[:, k, tsl(tl)], lambda k: [("xn", k, tl)], ba)
                        proj(wg, kg, 8, lambda k: xn[:, k, tsl(tl)], lambda k: [("xn", k, tl)], bb)
                        ti = tmpi()
                        act(tmp[ti][:, :], ps[ba][:, :], AF.Silu, [("ps", ba)], [("tmp", ti)])
                        vtt(hid[:, j, tsl(tl)], ps[bb][:, :], tmp[ti][:, :], ALU.mult,
                            [("ps", bb), ("tmp", ti)] + RALL, [("hid", j, tl)])
                for m in range(8):
                    if hh == 0 and m == 3:
                        norm_half(l, 1, goff)
                    if hh == 1 and m == 3 and hook is not None:
                        hook()
                    banks = [psb(), psb()]
                    for (k0, kc) in ((0, 8), (8, 8), (16, 6)):
                        wt, kw = wtile(l, (w_out_nm, k0, kc, m * 128, 128))
                        for tl in range(2):
                            for kk in range(kc):
                                j = k0 + kk
                                mm(ps[banks[tl]][:, :], wt[:, kk, :], hid[:, j, tsl(tl)], j == 0, j == NJ - 1,
                                   [kw, ("hid", j, tl)] + RALL, [("ps", banks[tl])])
                    for tl in range(2):
                        t = 2 * hh + tl
                        vstt(h[:, m, tsl(t)], ps[banks[tl]][:, :], 0.5, h[:, m, tsl(t)], ALU.mult, ALU.add,
                             [("ps", banks[tl]), ("h", m, t)], [("h", m, t)])

        def mixer(l, pre0=False, hook=None):
            fence(RALL + ["SQu"])
            gmix = 8
            vmemset(vaug[:, :, :, 64:128], 1.0, ["RVu"], ["vones"])
            only = os.environ.get("MK_ONLY", "qkvfg")
            for hh in range(2):
                if not (hh == 0 and pre0):
                    norm_half(l, hh, gmix)
                for c in range(4 if "q" in only else 0):
                    wt, kw = wtile(l, ("w_in", 0, 8, Q0 + c * 128, 128))
                    for tl in range(2):
                        t = 2 * hh + tl
                        b = psb()
                        proj(wt, kw, 8, lambda k: xn[:, k, tsl(tl)], lambda k: [("xn", k, tl)], b)
                        acopy(qT[:, c, tsl(t)], ps[b][:, :], [("ps", b), "RQu"], [("q", c, t, 0), ("q", c, t, 1)])
                for c in range(4 if "k" in only else 0):
                    wt, kw = wtile(l, ("w_in", 0, 8, K0 + c * 128, 128))
                    for tl in range(2):
                        t = 2 * hh + tl
                        b = psb()
                        proj(wt, kw, 8, lambda k: xn[:, k, tsl(tl)], lambda k: [("xn", k, tl)], b)
                        vcopy(kT[:, c, tsl(t)], ps[b][:, :], [("ps", b), "RKu"], [("k", c, 4 * t + i) for i in range(4)])
                for c in range(4 if "v" in only else 0):
                    wt, kw = wtile(l, ("w_in", 0, 8, V0 + c * 128, 128))
                    for il in range(8):
                        i = 8 * hh + il
                        tl = il // 4
                        b = psb()
                        for k in range(8):
                            mm(ps[b][:, 0:128], xn[:, k, il * 128:(il + 1) * 128], wt[:, k, :], k == 0, k == 7,
                               [kw, ("xn", k, tl)], [("ps", b)])
                        if il % 2 == 0:
                            vcopy(vaug[:, i, c, 0:64], ps[b][:, 0:64], [("ps", b), "RVu"], [("v", i, c, 0)])
                            vcopy(vaug[:, i, c, 128:192], ps[b][:, 64:128], [("ps", b), "RVu"], [("v", i, c, 1)])
                        else:
                            acopy(vaug[:, i, c, 0:64], ps[b][:, 0:64], [("ps", b), "RVu"], [("v", i, c, 0)])
                            acopy(vaug[:, i, c, 128:192], ps[b][:, 64:128], [("ps", b), "RVu"], [("v", i, c, 1)])
                wt, kw = wtile(l, ("w_in", 0, 8, F0, 8))
                b = psb()
                for il in range(8 if "f" in only else 0):
                    tl = il // 4
                    for k in range(8):
                        mm(ps[b][:, il * 8:(il + 1) * 8], xn[:, k, il * 128:(il + 1) * 128], wt[:, k, :], k == 0, k == 7,
                           [kw, ("xn", k, tl)], [("ps", b)])
                if "f" in only:
                    vtt(ftok[:, hh * 64:(hh + 1) * 64], ps[b][:, 0:64], sm(l, 164, 64), ALU.add,
                        [("ps", b), "small"], [("ftok", hh)])
                for c in range(4 if "g" in only else 0):
                    wa, ka = wtile(l, ("w_in", 0, 8, C0 + c * 128, 128))
                    wg, kg = wtile(l, ("w_in", 0, 8, C0 + CC + c * 128, 128))
                    for tl in range(2):
                        t = 2 * hh + tl
                        ba = psb(); bb = psb()
                        proj(wa, ka, 8, lambda k: xn[:, k, tsl(tl)], lambda k: [("xn", k, tl)], ba)
                        proj(wg, kg, 8, lambda k: xn[:, k, tsl(tl)], lambda k: [("xn", k, tl)], bb)
                        ti = tmpi()
                        act(tmp[ti][:, :], ps[bb][:, :], AF.Sigmoid, [("ps", bb)], [("tmp", ti)])
                        vtt(G[:, c, 30 + t * TS:30 + (t + 1) * TS], ps[ba][:, :], tmp[ti][:, :], ALU.mult,
                            [("ps", ba), ("tmp", ti)], [("G", c, t)])
            mixlvl = int(os.environ.get("MK_MIX", "9"))
            if mixlvl < 2:
                return
            act(sp_t[:, :], ftok[:, :], AF.Exp, [("ftok", 0), ("ftok", 1)], ["sp_t"], scale=-1.0)
            act(sp_t[:, :], sp_t[:, :], AF.Ln, ["sp_t"], ["sp_t"], scale=1.0, bias=1.0)
            b1 = psb(); b2 = psb()
            mm(ps[b1][:, 0:128], tri_f, sp_t[:, :], True, True, ["cst", "sp_t"], [("ps", b1)])
            mm(ps[b2][:, 0:128], ones_f, sp_t[:, :], True, True, ["cst", "sp_t"], [("ps", b2)])
            vcopy(tot[:, :], ps[b2][:, 0:128], [("ps", b2)], ["tot"])
            tot3 = tot[:, :].rearrange("p (i h) -> p i h", i=16)
            pre3 = pre[:, :].rearrange("p (i h) -> p i h", i=16)
            for hd in range(NH):
                P.op("dve", lambda e, hd=hd: e.tensor_tensor_scan(out=pre3[:, :, hd], data0=ones_f[:, 0:16], data1=tot3[:, :, hd],
                                                                  initial=0.0, op0=ALU.mult, op1=ALU.add),
                     r=["tot", "cst", "pre"], w=[("pre", hd)])
            vtt(pre[:, :], pre[:, :], tot[:, :], ALU.subtract, [("pre", hd) for hd in range(NH)] + ["tot"],
                ["pre"] + [("pre", hd) for hd in range(NH)])
            vtt(cstok[:, :], ps[b1][:, 0:128], pre[:, :], ALU.add, [("ps", b1), "pre"], ["cstok"])
            b3 = psb()
            mm(ps[b3][:, 0:128], e64_f, cstok[:, :], True, True, ["cst", "cstok"], [("ps", b3)])
            vcopy(refb[:, :], ps[b3][:, 0:128], [("ps", b3)], ["refb"])
            for bq in range(16):
                jr = 4 * (bq // 4) + 2
                vtt(tot[:, bq * 8:(bq + 1) * 8], refb[:, bq * 8:(bq + 1) * 8], refb[:, jr * 8:(jr + 1) * 8], ALU.subtract,
                    ["refb", "tot"], ["tot"])
            fence(XNKEYS + ["xnalias"])
            dma(INDv, ind_d[:, :], ["xnalias"], ["ind"])
            dma(MASKHv.rearrange("p h s -> p (h s)"), maskh_d[:, :], ["xnalias"], ["maskh"])
            bt = psb()
            P.op("pe", lambda e: e.transpose(ps[bt][:, 0:128], tot[:, :], ident_f), r=["tot", "cst"], w=[("ps", bt)])
            for hd in range(NH):
                P.op("dve", lambda e, hd=hd: e.tensor_scalar(out=AUGL[:, hd, :], in0=MASKHv[:, hd, :], scalar1=ps[bt][:, 0:1],
                                                             scalar2=-8.0, op0=ALU.mult, op1=ALU.mult),
                     r=[("ps", bt), "maskh", "xnalias"], w=[("augl", hd)])
            fence(["SQu"])
            cs3 = cstok[:, :].rearrange("p (i h) -> p i h", i=16)
            for j in range(4):
                for hd in range(NH):
                    col = (4 * j + 2) * 8 + hd
                    vts(bias[:, j, 0:4 * j + 4, hd], cs3[:, 0:4 * j + 4, hd], refb[:, col:col + 1], ALU.subtract,
                        ["cstok", "refb", "SQu"], [("bias", j)])
            if mixlvl < 3:
                return
            units = [(c, j, hx) for c in range(4) for j in range(4) for hx in range(2)]

            def qz_copy(u):
                c_, j_, hx_ = u
                pr = slice(hx_ * 64, (hx_ + 1) * 64)
                pcast(qzb[hx_][0][pr, :], qT[pr, c_, tsl(j_)], [("q", c_, j_, hx_), "RQu"], [("qz", hx_, 0)])
            qz_copy(units[0])
            for ui, (c, j, hx) in enumerate(units):
                if ui + 1 < len(units):
                    qz_copy(units[ui + 1])
                attn_unit(c, j, hx)
            fence(XNKEYS + ["xnalias", "SQu"])
            if mixlvl < 4:
                return
            norm_half(l, 0, gmix)
            fence(["RKu", "RVu"])
            for c in range(4):
                for k in range(CK):
                    if k % 2 == 0:
                        vts(diag[:, c, k, :], ident_f, sm(l, 40 + c * CK + k), ALU.mult,
                            ["cst", "small", "RKu", "RVu"], [("diag", c)])
                    else:
                        act(diag[:, c, k, :], ident_f, AF.Identity, ["cst", "small", "RKu", "RVu"], [("diag", c)],
                            scale=sm(l, 40 + c * CK + k))
            for t in (3, 2, 1, 0):
                cb = [psb() for _ in range(4)]
                for c in range(4):
                    rk = [("G", c, t)] + ([("G", c, t - 1)] if t > 0 else ["Gpad"])
                    for k in range(CK):
                        mm(ps[cb[c]][:, :], diag[:, c, k, :], G[:, c, t * TS + k:t * TS + k + TS], k == 0, k == CK - 1,
                           [("diag", c), "RKu", "RVu"] + rk, [("ps", cb[c])])
                for c in range(4):
                    act(sq[:, c, :], ps[cb[c]][:, :], AF.Square, [("ps", cb[c]), "small", "SQu"], ["sq"],
                        bias=sm(l, 32 + c), scale=1.0)
                ti = rstd_tile(None, [], 4, 1.0 / CC, sq_kw=True)
                for c in range(4):
                    t2 = tmpi()
                    vstt(tmp[t2][:, :], ps[cb[c]][:, :], sm(l, 32 + c), tmp[ti][:, :], ALU.add, ALU.mult,
                         [("ps", cb[c]), ("tmp", ti), "small"], [("tmp", t2)])
                    act(G[:, c, 30 + t * TS:30 + (t + 1) * TS], tmp[t2][:, :], AF.Silu, [("tmp", t2), "small"], [("G", c, t)],
                        scale=sm(l, 36 + c))
            if mixlvl < 5:
                return
            fence(["RKu", "RVu"])
            for hh in range(2):
                for m in range(8):
                    tt_ = []
                    for (wnm, gc0, src_is_attn) in (("w_attn_out", GA0, True), ("w_conv_out", GC0, False)):
                        wy, ky = wtile(l, (wnm, 0, 4, m * 128, 128))
                        wgt, kgt = wtile(l, ("w_in", 0, 8, gc0 + m * 128, 128))
                        row = []
                        for tl in range(2):
                            t = 2 * hh + tl
                            by = psb(); bg = psb()
                            if src_is_attn:
                                proj(wy, ky, 4, lambda k: qT[:, k, tsl(t)], lambda k: [("q", k, t, 0), ("q", k, t, 1), "RQu"], by)
                            else:
                                proj(wy, ky, 4, lambda k: G[:, k, 30 + t * TS:30 + (t + 1) * TS], lambda k: [("G", k, t)], by)
                            proj(wgt, kgt, 8, lambda k: xn[:, k, tsl(tl)], lambda k: [("xn", k, tl)], bg)
                            t1 = tmpi()
                            act(tmp[t1][:, :], ps[bg][:, :], AF.Sigmoid, [("ps", bg)], [("tmp", t1)])
                            if src_is_attn:
                                vtt(mrg[:, m, tsl(tl)], ps[by][:, :], tmp[t1][:, :], ALU.mult,
                                    [("ps", by), ("tmp", t1), "RKu"], [("mrgA", m, tl), ("mrg", m, tl)])
                            else:
                                vtt(tmp[t1][:, :], ps[by][:, :], tmp[t1][:, :], ALU.mult,
                                    [("ps", by), ("tmp", t1)], [("tmp", t1)])
                                vtt(mrg[:, m, tsl(tl)], mrg[:, m, tsl(tl)], tmp[t1][:, :], ALU.add,
                                    [("mrgA", m, tl), ("tmp", t1), "RKu"], [("mrg", m, tl), ("mrgA", m, tl)])
                for m in range(8):
                    if hh == 0 and m == 2:
                        norm_half(l, 1, gmix)
                    if hh == 1 and m == 2 and hook is not None:
                        hook()
                    wo, ko = wtile(l, ("w_out", 0, 8, m * 128, 128))
                    for tl in range(2):
                        t = 2 * hh + tl
                        b = psb()
                        proj(wo, ko, 8, lambda k: mrg[:, k, tsl(tl)], lambda k: [("mrg", k, tl), "RKu"], b)
                        vtt(h[:, m, tsl(t)], ps[b][:, :], h[:, m, tsl(t)], ALU.add, [("ps", b), ("h", m, t)], [("h", m, t)])

        def attn_unit(c, j, hx):
            hd = 2 * c + hx
            prow = slice(hx * 64, (hx + 1) * 64)
            drow = slice((1 - hx) * 64, (2 - hx) * 64)
            lo = 0 if hx == 0 else 64
            xb = 6 + (state["xb"] % 2); state["xb"] += 1
            zi = 0
            qz = qzb[hx][zi]
            nki = 4 * j + 4
            pend = []

            def qk(i):
                q0 = max(0, i - 4 * j)
                sbk = state["sb"] % 6; state["sb"] += 1
                mm(ps[sbk][:, q0 * 128:512], kT[:, c, i * 128:(i + 1) * 128],
                   qz[:, q0 * 128:512], True, False,
                   [("k", c, i), ("qz", hx, zi), "RKu"], [("ps", sbk)])
                diag_blk = i >= 4 * j
                mm(ps[sbk][:, q0 * 128:512], AUGL[:, hd, :], INDv[:, j * TS + q0 * 128:(j + 1) * TS], False, not diag_blk,
                   [("augl", hd), "ind", "xnalias"], [("ps", sbk)])
                if diag_blk:
                    mm(ps[sbk][:, q0 * 128:(q0 + 1) * 128], ident_b[:, :], tri_b[:, :], False, True,
                       ["ident_b", "tri_b"], [("ps", sbk)])
                pi = state["pt"] % 4; state["pt"] += 1
                act(ptl[pi][:, q0 * 128:512], ps[sbk][:, q0 * 128:512], AF.Exp,
                    [("ps", sbk), ("bias", j), "SQu"], [("pt", pi)],
                    scale=0.125, bias=bias[:, j, i, hd:hd + 1])
                return (i, q0, pi)

            def pv(item):
                i, q0, pi = item
                mm(ps[xb][:, q0 * 128:512], vaug[:, i, c, lo:lo + 128], ptl[pi][:, q0 * 128:512],
                   i == 0, i == nki - 1, [("v", i, c, hx), "vones", ("pt", pi), "RVu"], [("ps", xb)])

            for i in range(nki):
                pend.append(qk(i))
                if len(pend) > 3:
                    pv(pend.pop(0))
            while pend:
                pv(pend.pop(0))
            ti = tmpi()
            vrecip(tmp[ti][prow, :], ps[xb][drow, :], [("ps", xb)], [("tmp", ti)])
            vtt(qT[prow, c, tsl(j)], ps[xb][prow, :], tmp[ti][prow, :], ALU.mult,
                [("ps", xb), ("tmp", ti), "RQu"], [("q", c, j, hx)])

        def ple(l, pre0=False):
            for hh in range(2):
                if not (hh == 0 and pre0):
                    norm_half(l, hh, 24)
                for k in range(2):
                    si = stage(pT_d[l, hh, :, k, :], 1024)
                    pcast(pbuf[k][:, :], stg[si][:, 0:1024], [("stg", si)], [("pbuf", k)])
                for m in range(8):
                    wg, kg = wtile(l, ("w_ple_gate", 0, 8, m * 128, 128))
                    wp, kp = wtile(l, ("w_ple_proj", 0, 2, m * 128, 128))
                    for tl in range(2):
                        t = 2 * hh + tl
                        bg = psb(); bp = psb()
                        proj(wg, kg, 8, lambda k: xn[:, k, tsl(tl)], lambda k: [("xn", k, tl)], bg)
                        proj(wp, kp, 2, lambda k: pbuf[k][:, tsl(tl)], lambda k: [("pbuf", k)], bp)
                        t1 = tmpi()
                        act(tmp[t1][:, :], ps[bg][:, :], AF.Sigmoid, [("ps", bg)], [("tmp", t1)])
                        vtt(tmp[t1][:, :], ps[bp][:, :], tmp[t1][:, :], ALU.mult, [("ps", bp), ("tmp", t1)], [("tmp", t1)])
                        vtt(h[:, m, tsl(t)], tmp[t1][:, :], h[:, m, tsl(t)], ALU.add, [("tmp", t1), ("h", m, t)], [("h", m, t)])

        for l in range(depth):
            P.phase = l
            parts = os.environ.get("MK_PARTS", "fmgp")
            full = parts == "fmgp"
            if "f" in parts:
                ffn(l, "w_ff1_in", "w_ff1_out", 0, False, (lambda l=l: norm_half(l, 0, 8)) if full else None)
            if "m" in parts:
                mixer(l, full, (lambda l=l: norm_half(l, 0, 16)) if full else None)
            if "g" in parts:
                ffn(l, "w_ff2_in", "w_ff2_out", 16, full, (lambda l=l: norm_half(l, 0, 24)) if full else None)
            if "p" in parts:
                ple(l, full)

        P.phase = depth
        gf = SM_L * L
        for t in range(4):
            ti = rstd_tile(h[:, :, tsl(t)], [("h", k, t) for k in range(8)], 8, 1.0 / D)
            for k in range(8):
                vstt(h[:, k, tsl(t)], h[:, k, tsl(t)], small[:, gf + k:gf + k + 1], tmp[ti][:, :], ALU.mult, ALU.mult,
                     [("h", k, t), ("tmp", ti), "small"], [("h", k, t)])
            dma(outT_d.rearrange("(c p) s -> p c s", p=128)[:, :, tsl(t)], h[:, :, tsl(t)],
                [("h", k, t) for k in range(8)], [("out", t)])
        P.op("sp", None, r=[("out", t) for t in range(4)])
        P.emit()
    return nc


def _pack(inputs, depth):
    specs, offs, TPP = layer_tile_specs()
    wts = np.zeros((max(depth, 1), 128, TPP), np.float32)
    for l in range(depth):
        for sp in specs:
            nm, k0, kc, c0, ncol = sp
            W = inputs[nm][l]
            blk = W[k0 * 128:(k0 + kc) * 128, c0:c0 + ncol].reshape(kc, 128, ncol).transpose(1, 0, 2).reshape(128, kc * ncol)
            wts[l, :, offs[sp]:offs[sp] + kc * ncol] = blk
    small = np.zeros((128, SM_TOT), np.float32)

    def fm(v, nchunk):
        return np.asarray(v, np.float32).reshape(nchunk, 128).T
    for l in range(L):
        o = l * SM_L
        small[:, o + 0:o + 8] = fm(inputs["g_ff1"][l], 8)
        small[:, o + 8:o + 16] = fm(inputs["g_mix"][l], 8)
        small[:, o + 16:o + 24] = fm(inputs["g_ff2"][l], 8)
        small[:, o + 24:o + 32] = fm(inputs["g_ple"][l], 8)
        small[:, o + 32:o + 36] = fm(inputs["conv_b"][l], 4)
        small[:, o + 36:o + 40] = fm(inputs["g_conv"][l], 4)
        cw = np.asarray(inputs["conv_w"][l], np.float32)
        small[:, o + 40:o + 164] = cw.reshape(CK, 4, 128).transpose(2, 1, 0).reshape(128, 4 * CK)
        small[:, o + 164:o + 228] = np.tile(np.asarray(inputs["b_f"][l], np.float32)[None, :], (128, 8))
    small[:, SM_L * L:] = fm(inputs["g_final"], 8)
    consts = np.zeros((128, 512), np.float32)
    consts[:, 0:128] = np.eye(128, dtype=np.float32)
    consts[:, 128:256] = np.triu(np.ones((128, 128), np.float32))
    consts[64, 256:384] = 1.0
    consts[:, 384:512] = 1.0
    return wts, small, consts


def kernel(**inputs):
    depth = int(os.environ.get("MK_DEPTH", L))
    inputs = {k: np.asarray(v) for k, v in inputs.items()}
    wts, small, consts = _pack(inputs, depth)
    x = inputs["x"].astype(np.float32, copy=False)
    p = inputs["p"].astype(np.float32, copy=False)
    nc = build_program(depth)
    import ml_dtypes
    pidx = np.arange(128)
    ind = (pidx[:, None] // 8 == (np.arange(2048)[None, :] // 128)).astype(ml_dtypes.bfloat16)
    maskh = np.repeat((pidx[:, None] % 8 == np.arange(8)[None, :]).astype(ml_dtypes.bfloat16), 128, axis=1)
    in_maps = []
    for b in range(8):
        xT = np.ascontiguousarray(x[b].T)
        pT = np.ascontiguousarray(p[:, b].reshape(L, 2, 1024, 2, 128).transpose(0, 1, 4, 3, 2))
        in_maps.append({"xT": xT, "pT": pT, "wts": wts, "small": small, "consts": consts, "ind": ind, "maskh": maskh})
    res = run_bass_kernel_spmd(nc, in_maps, core_ids=list(range(8)))
    out = np.stack([np.ascontiguousarray(r["outT"].T) for r in res.results], axis=0)
    return out.astype(np.float32, copy=False)
```
